# Optimizing a Trainium2 kernel written in Bass

```python
import jax, jax.numpy as jnp
from jax import lax
import numpy as np

D_MODEL = 1024
BATCH = 4
SEQ = 4096
DEPTH = 1

CONV_WIDTH = 512
CONV_GROUPS = 8
CONV_K = 3
GLA_HEADS = 4
GLA_DK = 64
GLA_DV = 128
GLA_KEY_WIDTH = GLA_HEADS * GLA_DK
GLA_VAL_WIDTH = GLA_HEADS * GLA_DV
GATE_RANK = 16
GATE_NORMALIZER = 16.0
CHUNK = 64
MIX_WIDTH = CONV_WIDTH + GLA_VAL_WIDTH
IN_SIZES = [CONV_WIDTH, CONV_WIDTH, CONV_WIDTH,
            GLA_KEY_WIDTH, GLA_KEY_WIDTH, GLA_VAL_WIDTH,
            GLA_VAL_WIDTH, GATE_RANK]
IN_COLS = sum(IN_SIZES)
IN_SPLITS = np.cumsum(IN_SIZES)[:-1].tolist()
D_FF = 2816
N_ADA = 9
EPS = 1e-6

kernel_name = "hybrid_conv_gla_macaron_adaln"


def rms_norm(x, gain):
    xf = x.astype(jnp.float32)
    y = xf * lax.rsqrt(jnp.mean(xf * xf, axis=-1, keepdims=True) + EPS)
    return (y * gain.astype(jnp.float32)).astype(x.dtype)


def modulate(h, shift, scale):
    return h * (1.0 + scale[:, None, :]) + shift[:, None, :]


def swiglu_ffn(h, w_in, w_out):
    gate, up = jnp.split(h @ w_in, 2, axis=-1)
    return (jax.nn.silu(gate) * up) @ w_out


def causal_short_conv(u, w):
    T = u.shape[1]
    up = jnp.pad(u, ((0, 0), (CONV_K - 1, 0), (0, 0)))
    y = up[:, 0:T, :] * w[0]
    for k in range(1, CONV_K):
        y = y + up[:, k:k + T, :] * w[k]
    return y


def gla_chunked(q, k, v, log_g):
    B, H, T, DK = q.shape
    DV = v.shape[-1]
    N = T // CHUNK

    def to_chunks(a):
        return a.reshape(B, H, N, CHUNK, a.shape[-1]).transpose(2, 0, 1, 3, 4)

    qc, kc, vc, gc = to_chunks(q), to_chunks(k), to_chunks(v), to_chunks(log_g)
    bc = jnp.cumsum(gc, axis=3)
    causal = jnp.tril(jnp.ones((CHUNK, CHUNK), dtype=bool))[:, :, None]

    def step(S, inp):
        q_, k_, v_, b_ = inp
        o_inter = jnp.einsum('bhik,bhkv->bhiv', q_ * jnp.exp(b_), S)
        rel = b_[:, :, :, None, :] - b_[:, :, None, :, :]
        decay = jnp.where(causal, jnp.exp(jnp.where(causal, rel, 0.0)), 0.0)
        scores = jnp.einsum('bhik,bhijk,bhjk->bhij', q_, decay, k_)
        o_intra = jnp.einsum('bhij,bhjv->bhiv', scores, v_)
        b_last = b_[:, :, -1:, :]
        S_new = (jnp.exp(b_last[:, :, 0, :])[..., None] * S
                 + jnp.einsum('bhjk,bhjv->bhkv', k_ * jnp.exp(b_last - b_), v_))
        return S_new, o_inter + o_intra

    S0 = jnp.zeros((B, H, DK, DV), jnp.float32)
    _, o = lax.scan(step, S0, (qc, kc, vc, bc))
    return o.transpose(1, 2, 0, 3, 4).reshape(B, H, T, DV)


def token_mixer(h, w_in, conv_w, w_gk2, b_gk, gla_norm, w_out):
    Bsz, T, _ = h.shape
    proj = h @ w_in
    cb, cc, cv, q, k, v, g_out, gk_low = jnp.split(proj, IN_SPLITS, axis=-1)
    y_conv = cb * causal_short_conv(cc * cv, conv_w)
    log_g = jax.nn.log_sigmoid((gk_low @ w_gk2 + b_gk).astype(jnp.float32)) / GATE_NORMALIZER

    def heads(a, d):
        return a.reshape(Bsz, T, GLA_HEADS, d).transpose(0, 2, 1, 3).astype(jnp.float32)

    o = gla_chunked(heads(q, GLA_DK) * (GLA_DK ** -0.5), heads(k, GLA_DK),
                    heads(v, GLA_DV), heads(log_g, GLA_DK))
    o = rms_norm(o, gla_norm)
    o = o.transpose(0, 2, 1, 3).reshape(Bsz, T, GLA_VAL_WIDTH).astype(h.dtype)
    y_gla = o * jax.nn.silu(g_out)
    return jnp.concatenate([y_conv, y_gla], axis=-1) @ w_out


def setup_inputs(seed: int = 0) -> dict:
    key = jax.random.key(seed)
    ks = jax.random.split(key, 20)
    f32 = jnp.float32
    nrm = lambda k, shape, s: jax.random.normal(k, shape, f32) * s
    L, D = DEPTH, D_MODEL
    return {
        "x": nrm(ks[0], (BATCH, SEQ, D), 1.0),
        "c": nrm(ks[1], (BATCH, D), 1.0),
        "w_ada": nrm(ks[2], (L, D, N_ADA * D), 0.5 * D ** -0.5),
        "b_ada": nrm(ks[3], (L, N_ADA * D), 0.01),
        "norm_ffn1": 1.0 + nrm(ks[4], (L, D), 0.02),
        "w_ffn1_in": nrm(ks[5], (L, D, 2 * D_FF), D ** -0.5),
        "w_ffn1_out": nrm(ks[6], (L, D_FF, D), D_FF ** -0.5),
        "norm_mix": 1.0 + nrm(ks[7], (L, D), 0.02),
        "w_mix_in": nrm(ks[8], (L, D, IN_COLS), D ** -0.5),
        "conv_w": nrm(ks[9], (L, CONV_K, CONV_WIDTH), CONV_K ** -0.5),
        "w_gk2": nrm(ks[10], (L, GATE_RANK, GLA_KEY_WIDTH), GATE_RANK ** -0.5),
        "b_gk": nrm(ks[11], (L, GLA_KEY_WIDTH), 0.01),
        "gla_norm": 1.0 + nrm(ks[12], (L, GLA_DV), 0.02),
        "w_mix_out": nrm(ks[13], (L, MIX_WIDTH, D), MIX_WIDTH ** -0.5),
        "norm_ffn2": 1.0 + nrm(ks[14], (L, D), 0.02),
        "w_ffn2_in": nrm(ks[15], (L, D, 2 * D_FF), D ** -0.5),
        "w_ffn2_out": nrm(ks[16], (L, D_FF, D), D_FF ** -0.5),
        "norm_final": 1.0 + nrm(ks[17], (D,), 0.02),
    }


def reference(x, c, w_ada, b_ada, norm_ffn1, w_ffn1_in, w_ffn1_out, norm_mix,
              w_mix_in, conv_w, w_gk2, b_gk, gla_norm, w_mix_out, norm_ffn2,
              w_ffn2_in, w_ffn2_out, norm_final):
    c_act = jax.nn.silu(c)
    for l in range(DEPTH):
        ada = c_act @ w_ada[l] + b_ada[l]
        sh1, sc1, g1, sh2, sc2, g2, sh3, sc3, g3 = jnp.split(ada, N_ADA, axis=-1)
        h = modulate(rms_norm(x, norm_ffn1[l]), sh1, sc1)
        x = x + 0.5 * g1[:, None, :] * swiglu_ffn(h, w_ffn1_in[l], w_ffn1_out[l])
        h = modulate(rms_norm(x, norm_mix[l]), sh2, sc2)
        x = x + g2[:, None, :] * token_mixer(h, w_mix_in[l], conv_w[l], w_gk2[l],
                                             b_gk[l], gla_norm[l], w_mix_out[l])
        h = modulate(rms_norm(x, norm_ffn2[l]), sh3, sc3)
        x = x + 0.5 * g3[:, None, :] * swiglu_ffn(h, w_ffn2_in[l], w_ffn2_out[l])
    return rms_norm(x, norm_final)
```

```python
from contextlib import ExitStack
import numpy as np
import concourse.bass as bass
import concourse.mybir as mybir
from concourse.bass_utils import run_bass_kernel_spmd

F32 = mybir.dt.float32
BF16 = mybir.dt.bfloat16
AF = mybir.ActivationFunctionType
ALU = mybir.AluOpType

D = 1024
KD = 8
DFF = 2816
NF = 22
BATCH = 4
SEQ = 4096
NCORES = 8
EPS = 1e-6
NV = 129
XW = 520

ENGS = ['pe', 'act', 'dve', 'pool', 'sp']
EPOCH = 30000

STAGES = None


class Op:
    __slots__ = ('eng', 'fn', 'deps', 'dma_key', 'signals', 'val')


class Prog:
    def __init__(self):
        self.streams = {e: [] for e in ENGS}
        self.last_w = {}
        self.readers = {}
        self.dma_count = {}
        self.nops = 0
        self.pending_fence = {}

    def fence(self):
        if not self.enabled:
            return
        last = [self.streams[e][-1] for e in ('pe', 'act', 'dve') if self.streams[e]]
        for e in ENGS:
            self.pending_fence[e] = list(last)

    enabled = True

    off = set()

    def stage(self, name):
        if name in self.off:
            self.enabled = False
            return
        self.enabled = STAGES is None or name in STAGES or (name[0] == 'm' and name[1:].isdigit() and 'mix_tiles' in STAGES and not any(
            x[0] == 'm' and x[1:].isdigit() for x in STAGES))

    def add(self, eng, fn, reads=(), writes=(), dma_key=None, nofence=False):
        if not self.enabled:
            return None
        op = Op()
        op.eng = eng
        op.fn = fn
        op.dma_key = dma_key
        op.signals = False
        op.val = 0
        psr = [r for r in reads if isinstance(r, tuple) and r[0] == 'ps']
        if psr:
            writes = list(writes) + psr
        deps = set()
        for r in reads:
            w = self.last_w.get(r)
            if w is not None:
                deps.add(w)
        for r in writes:
            w = self.last_w.get(r)
            if w is not None:
                deps.add(w)
            rd = self.readers.get(r)
            if rd:
                deps.update(rd.values())
        if eng == 'pe' and dma_key is None:
            deps = {d for d in deps if not (d.eng == 'pe' and d.dma_key is None)}
        if self.pending_fence.get(eng) and not nofence:
            for d in self.pending_fence[eng]:
                if not (eng == 'pe' and d.eng == 'pe'):
                    deps.add(d)
            self.pending_fence[eng] = None
        op.deps = deps
        for d in deps:
            d.signals = True
        for r in reads:
            rk = eng if dma_key is None else ('dma', self.nops)
            self.readers.setdefault(r, {})[rk] = op
        for r in writes:
            self.last_w[r] = op
            self.readers[r] = {}
        if dma_key is not None:
            self.dma_count[dma_key] = self.dma_count.get(dma_key, 0) + 16
            op.val = self.dma_count[dma_key]
        self.streams[eng].append(op)
        self.nops += 1
        return op

    def emit(self, nc, st):
        nep = {}
        for e in ENGS:
            cnt = 0
            for op in self.streams[e]:
                if op.dma_key is None and op.signals:
                    cnt += 1
                    op.val = cnt
            nep[e] = cnt // EPOCH + 1
        sems = {e: [st.enter_context(nc.semaphore(f"s_{e}{i}")) for i in range(nep[e])] for e in ENGS}
        dsems = {k: st.enter_context(nc.semaphore(f"d_{k}")) for k in self.dma_count}
        block = st.enter_context(nc.Block())

        def sem_of(d):
            if d.dma_key is not None:
                return ('d', d.dma_key), dsems[d.dma_key], d.val
            ep = (d.val - 1) // EPOCH
            return (d.eng, ep), sems[d.eng][ep], d.val - ep * EPOCH

        streams = self.streams
        dma_count = self.dma_count

        def mk(e):
            def body(engobj):
                waited = {}
                for op in streams[e]:
                    need = {}
                    for d in op.deps:
                        k, s, v = sem_of(d)
                        if k not in need or need[k][1] < v:
                            need[k] = (s, v)
                    for k, (s, v) in need.items():
                        if waited.get(k, 0) >= v:
                            continue
                        engobj.wait_ge(s, v)
                        waited[k] = v
                    ins = op.fn(engobj)
                    if op.dma_key is not None:
                        ins.then_inc(dsems[op.dma_key], 16)
                    elif op.signals:
                        k, s, v = sem_of(op)
                        ins.then_inc(s, 1)
                if e == 'sp':
                    for k, c in dma_count.items():
                        engobj.wait_ge(dsems[k], c)
            return body

        block.tensor(mk('pe'))
        block.scalar(mk('act'))
        block.vector(mk('dve'))
        block.gpsimd(mk('pool'))
        block.sync(mk('sp'))


def build(seq):
    TT = seq // 2
    NT = TT // 512
    nc = bass.Bass("TRN2", target_bir_lowering=False)

    def din(name, shape):
        return nc.dram_tensor(name, shape, F32, kind="ExternalInput").ap()

    xT_d = din("xT", [D, TT])
    xTp_d = din("xTp", [D, TT])
    vecs_d = din("vecs", [128, NV])
    wada_d = din("w_ada", [D, 9 * D])
    w1i_d = din("w_ffn1_in", [D, 2 * DFF])
    w1o_d = din("w_ffn1_out", [DFF, D])
    wmi_d = din("w_mix_in", [D, 3088])
    wgk2_d = din("wgk2a", [17, 256])
    wmo_d = din("w_mix_out", [D, D])
    w2i_d = din("w_ffn2_in", [D, 2 * DFF])
    w2o_d = din("w_ffn2_out", [DFF, D])
    tris_d = din("tris", [128, 128])
    triu_d = din("triu", [128, 128])
    mask_d = din("mask", [128, 256])
    out_d = nc.dram_tensor("outT", [D, TT], F32, kind="ExternalOutput").ap()

    P = Prog()
    with ExitStack() as st:
        def sb(name, shape, dt):
            return st.enter_context(nc.sbuf_tensor(name, shape, dt))

        xT = sb("xT_sb", [128, KD, TT], F32)
        A_raw = sb("arenaA", [128, 4 * TT], F32)
        B_raw = sb("arenaB", [128, max(4 * TT, 8192)], F32)
        D_raw = sb("arenaD", [128, 9760], F32)
        WA = sb("wada_slots", [128, 2, KD, 128], BF16)
        WS = sb("wslots", [128, 2, KD, 512], BF16)
        TMP = sb("tmp", [128, 6, 512], F32)
        vecs = sb("vecs_sb", [128, NV], F32)
        ada = sb("ada_sb", [128, 72], F32)
        prm = sb("prm_sb", [128, 64], F32)
        cact = sb("cact", [128, KD], BF16)
        ones_bf = sb("ones_bf", [128, 128], BF16)
        tris = sb("tris_sb", [128, 128], F32)
        triu = sb("triu_sb", [128, 128], F32)
        mask = sb("mask_sb", [128, 256], F32)
        wgk2 = sb("wgk2_sb", [32, 256], F32)
        wgk = sb("wgk_sb", [128, KD, 16], BF16)
        sqt = sb("sqt", [128, 2, 512], BF16)
        small = sb("small", [128, 64], F32)
        spst = sb("spst", [128, XW], F32)

        PS = [st.enter_context(nc.psum_tensor(f"ps{i}", [128, 512], F32)) for i in range(8)]
        ps_state = {'i': 0}

        def ps_next():
            i = ps_state['i'] % 8
            ps_state['i'] += 1
            return PS[i], ('ps', i)

        tmp_state = {'i': 0}

        def tmp_next():
            i = tmp_state['i'] % 4
            tmp_state['i'] += 1
            return TMP[:, i, :], ('tmp', i)

        A_bf = A_raw[:, :].bitcast(BF16).rearrange("p (k t) -> p k t", k=KD)
        B_bf = B_raw[:, 0:4 * TT].bitcast(BF16).rearrange("p (k t) -> p k t", k=KD)
        oTb = [B_raw[:, b * 2048:(b + 1) * 2048].rearrange("p (h t) -> p h t", h=4) for b in range(2)]
        hTt1 = B_raw[:, 4096:6144].bitcast(BF16).rearrange("p (k t) -> p k t", k=KD)
        WS3 = B_raw[:, 6144:8192].bitcast(BF16).rearrange("p (k c) -> p k c", k=KD)

        def wslot(s):
            return WS3 if s == 2 else WS[:, s]
        D_bf = D_raw[:, :].bitcast(BF16)
        wo_buf = [D_bf[:, i * 6144:(i + 1) * 6144].rearrange("p (f d) -> p f d", f=6) for i in range(2)]
        o0 = 0
        hTt = D_bf[:, o0:o0 + 4096].rearrange("p (k t) -> p k t", k=KD)
        o0 += 4096
        qtT = D_bf[:, o0:o0 + 1024].rearrange("p (a t) -> p a t", a=2)
        o0 += 1024
        ktT = D_bf[:, o0:o0 + 1024].rearrange("p (a t) -> p a t", a=2)
        o0 += 1024
        khat = D_bf[:, o0:o0 + 1024].rearrange("p (a t) -> p a t", a=4)
        o0 += 1024
        vtok = D_bf[:, o0:o0 + 2048].rearrange("p (a t) -> p a t", a=4)
        o0 += 2048
        scm = D_bf[:, o0:o0 + 1024].rearrange("p (a t) -> p a t", a=4)
        o0 += 1024
        S_bf = D_bf[:, o0:o0 + 1024].rearrange("p (a t) -> p a t", a=4)
        o0 += 1024
        f0 = o0 // 2
        bT_sb = D_raw[:, f0:f0 + 1024].rearrange("p (a t) -> p a t", a=2)
        f0 += 1024
        NB = 3
        sp_bufs = [D_raw[:, f0 + 256 * i:f0 + 256 * (i + 1)] for i in range(NB)]
        f0 += 256 * NB
        er_bufs = [D_raw[:, f0 + 256 * i:f0 + 256 * (i + 1)] for i in range(NB)]
        f0 += 256 * NB
        ubuf = D_raw[:, f0:f0 + 516]
        f0 += 516
        S_sb = D_raw[:, f0:f0 + 512].rearrange("p (a t) -> p a t", a=2)
        f0 += 512
        gk_aug = D_raw[0:32, f0:f0 + 512]
        f0 += 512
        assert f0 <= 9760, f0
        V_C, V_BADA, V_N1, V_NM, V_N2, V_NF, V_CW, V_GN, V_FLAG = 0, 8, 80, 88, 96, 104, 112, 124, 125
        P_A1, P_A2, P_A3, P_GH1, P_GH3, P_EPS = 0, 8, 16, 24, 32, 40
        P_CB01 = 44
        P_UH = 52

        P.add('sp', lambda e: e.dma_start(out=vecs[:, :], in_=vecs_d[:, :]), writes=['vecs'], dma_key='vecs')
        for k in range(KD):
            P.add('sp', lambda e, k=k: e.dma_start(out=xT[:, k, :], in_=xTp_d[k * 128:(k + 1) * 128, :]),
                  writes=[('xT', k, t) for t in range(NT)], dma_key=f'x{k}')
        P.add('sp', lambda e: e.dma_start(out=tris[:, :], in_=tris_d[:, :]), writes=['tris'], dma_key='c0')
        P.add('sp', lambda e: e.dma_start(out=triu[:, :], in_=triu_d[:, :]), writes=['triu'], dma_key='c1')
        P.add('sp', lambda e: e.dma_start(out=mask[:, :], in_=mask_d[:, :]), writes=['mask'], dma_key='c2')
        P.add('sp', lambda e: e.dma_start(out=wgk2[0:17, :], in_=wgk2_d[:, :]), writes=['wgk2'], dma_key='c3')
        P.add('dve', lambda e: e.memset(ones_bf[:, :], 1.0), writes=['ones'])
        P.add('dve', lambda e: e.memset(prm[:, P_EPS:P_EPS + 1], EPS), writes=['eps'])
        P.add('act', lambda e: e.activation(out=cact[:, :], in_=vecs[:, V_C:V_C + 8], func=AF.Silu),
              reads=['vecs'], writes=['cact'])

        ws_state = {'n': 0, 'nslots': 2}

        def ws_alloc():
            s = ws_state['n'] % ws_state['nslots']
            ws_state['n'] += 1
            return s

        def load_cols(slot, half, src, c0, ncols, off=0):
            co = half * 256 + off
            dst = wslot(slot)[:, :, co: co + ncols]
            srcv = src.rearrange("(k p) c -> p k c", p=128)[:, :, c0:c0 + ncols]
            qs = list(range(co // 128, (co + ncols) // 128))
            P.add('pool', lambda e: e.dma_start(out=dst, in_=srcv),
                  writes=[('ws', slot, q) for q in qs], dma_key=f'ws{slot}_{qs[0]}', nofence=(slot != 2))

        def wsk(slot, qs):
            return [('ws', slot, q) for q in qs]

        ada_state = {'blk': 0, 'c128': 16}
        wa_state = {'n': 0}

        def ada_block():
            blk = ada_state['blk']
            ada_state['blk'] += 1
            s = ws_alloc()
            load_cols(s, 0, wada_d, blk * 512, 256)
            load_cols(s, 1, wada_d, blk * 512 + 256, 256)
            pa, pak = ps_next()

            def fn(e, s=s, pa=pa):
                ins = None
                for cc in range(4):
                    for k in range(KD):
                        ins = e.matmul(pa[:, cc:cc + 1], lhsT=wslot(s)[:, k, cc * 128:(cc + 1) * 128], rhs=cact[:, k:k + 1],
                                       start=(k == 0), stop=(k == KD - 1))
                return ins
            P.add('pe', fn, reads=wsk(s, range(4)) + ['cact'], writes=[pak])
            P.add('dve', lambda e, blk=blk, pa=pa: e.tensor_tensor(out=ada[:, blk * 4:blk * 4 + 4], in0=pa[:, 0:4],
                                                                   in1=vecs[:, V_BADA + blk * 4:V_BADA + blk * 4 + 4], op=ALU.add),
                  reads=[pak, 'vecs'], writes=[('ada', blk * 4 + i) for i in range(4)])

        def ada_chunk():
            c = ada_state['c128']
            ada_state['c128'] += 1
            s = wa_state['n'] % 2
            wa_state['n'] += 1
            srcv = wada_d.rearrange("(k p) c -> p k c", p=128)[:, :, c * 128:(c + 1) * 128]
            P.add('pool', lambda e: e.dma_start(out=WA[:, s, :, :], in_=srcv), writes=[('wa', s)], dma_key=f'wa{s}')
            pa, pak = ps_next()

            def fn(e):
                ins = None
                for k in range(KD):
                    ins = e.matmul(pa[:, 0:1], lhsT=WA[:, s, k, :], rhs=cact[:, k:k + 1], start=(k == 0), stop=(k == KD - 1))
                return ins
            P.add('pe', fn, reads=[('wa', s), 'cact'], writes=[pak])
            P.add('dve', lambda e: e.tensor_tensor(out=ada[:, c:c + 1], in0=pa[:, 0:1], in1=vecs[:, V_BADA + c:V_BADA + c + 1], op=ALU.add),
                  reads=[pak, 'vecs'], writes=[('ada', c)])

        def ada2():
            ada_chunk()
            ada_chunk()

        def ada_keys(n):
            return [('ada', 8 * n + i) for i in range(8)]

        def make_A(col, n_sc, vcol):
            P.add('dve', lambda e: e.scalar_tensor_tensor(out=prm[:, col:col + 8], in0=ada[:, n_sc * 8:n_sc * 8 + 8], scalar=1.0,
                                                          in1=vecs[:, vcol:vcol + 8], op0=ALU.add, op1=ALU.mult),
                  reads=ada_keys(n_sc) + ['vecs'], writes=[('prm', col)])

        def make_half(col, n_g):
            P.add('dve', lambda e: e.tensor_scalar(out=prm[:, col:col + 8], in0=ada[:, n_g * 8:n_g * 8 + 8], scalar1=0.5, scalar2=None,
                                                   op0=ALU.mult),
                  reads=ada_keys(n_g), writes=[('prm', col)])

        def norm_tile(t, dst, dst_key, acol, sh_n, sh_reads):
            cols = slice(t * 512, (t + 1) * 512)
            ps, psk = ps_next()
            for k in range(KD):
                b = k % 2
                P.add('act', lambda e, k=k, b=b: e.activation(out=sqt[:, b, :], in_=xT[:, k, cols], func=AF.Square),
                      reads=[('xT', k, t)], writes=[('sqt', b)])
                P.add('pe', lambda e, k=k, b=b: e.matmul(ps[:, :], lhsT=ones_bf[:, :], rhs=sqt[:, b, :],
                                                         start=(k == 0), stop=(k == KD - 1)),
                      reads=[('sqt', b), 'ones'], writes=[psk])
            rs, rsk = TMP[:, 4, :], ('tmp', 4)
            P.add('act', lambda e: e.activation(out=rs, in_=ps[:, :], func=AF.Ln, scale=1.0 / D, bias=prm[:, P_EPS:P_EPS + 1]),
                  reads=[psk, 'eps'], writes=[rsk])
            P.add('act', lambda e: e.activation(out=rs, in_=rs, func=AF.Exp, scale=-0.5), reads=[rsk], writes=[rsk])
            for k in range(KD):
                tm, tmk = tmp_next()
                P.add('dve', lambda e, k=k, tm=tm: e.scalar_tensor_tensor(out=tm, in0=xT[:, k, cols], scalar=prm[:, acol + k:acol + k + 1],
                                                                          in1=rs, op0=ALU.mult, op1=ALU.mult),
                      reads=[('xT', k, t), ('prm', acol), rsk], writes=[tmk])
                P.add('act', lambda e, k=k, tm=tm: e.activation(out=dst(k), in_=tm, func=AF.Identity,
                                                                bias=ada[:, sh_n * 8 + k:sh_n * 8 + k + 1], scale=1.0),
                      reads=[tmk] + sh_reads, writes=[dst_key(k)])

        def ffn_norm(t, acol, sh_n):
            norm_tile(t, lambda k, t=t: A_bf[:, k, t * 512:(t + 1) * 512], lambda k, t=t: ('hT', k, t),
                      acol, sh_n, ada_keys(sh_n))

        def ffn(win_d, wout_d, acol, sh_n, ghcol, extra, pre_out0, do_norm=True, post_tile=None):
            if do_norm:
                for t in range(NT):
                    ffn_norm(t, acol, sh_n)
            groups = [(0, 6), (6, 12), (12, 18), (18, 22)]
            slot_ctr = {'n': 0}
            at_slot = {}

            def in_pair(pr):
                s = ws_alloc()
                f0_ = pr * 2
                load_cols(s, 0, win_d, f0_ * 128, 256)
                load_cols(s, 1, win_d, DFF + f0_ * 128, 256)
                for fi in range(2):
                    f = f0_ + fi
                    sl = slot_ctr['n'] % 8
                    slot_ctr['n'] += 1
                    at_slot[f] = sl
                    for t in range(NT):
                        cols = slice(t * 512, (t + 1) * 512)
                        pg, pgk = ps_next()
                        pu, puk = ps_next()

                        def fn(e, s=s, fi=fi, cols=cols, pg=pg, pu=pu):
                            ins = None
                            for k in range(KD):
                                ins = e.matmul(pg[:, :], lhsT=wslot(s)[:, k, fi * 128:(fi + 1) * 128], rhs=A_bf[:, k, cols],
                                               start=(k == 0), stop=(k == KD - 1))
                            for k in range(KD):
                                ins = e.matmul(pu[:, :], lhsT=wslot(s)[:, k, 256 + fi * 128:256 + (fi + 1) * 128], rhs=A_bf[:, k, cols],
                                               start=(k == 0), stop=(k == KD - 1))
                            return ins
                        P.add('pe', fn, reads=wsk(s, [fi, 2 + fi]) + [('hT', k, t) for k in range(KD)], writes=[pgk, puk])
                        tm, tmk = tmp_next()
                        P.add('act', lambda e, tm=tm, pg=pg: e.activation(out=tm, in_=pg[:, :], func=AF.Silu), reads=[pgk], writes=[tmk])
                        P.add('dve', lambda e, tm=tm, pu=pu, sl=sl, cols=cols: e.tensor_tensor(out=B_bf[:, sl, cols], in0=tm, in1=pu[:, :], op=ALU.mult),
                              reads=[tmk, puk], writes=[('aT', sl, t)])

            def load_wout(g):
                fa, fb = groups[g]
                for i in range((fb - fa) // 2):
                    dst = wo_buf[g % 2][:, 2 * i:2 * i + 2, :]
                    srcv = wout_d[(fa + 2 * i) * 128:(fa + 2 * i + 2) * 128, :].rearrange("(f p) d -> p f d", p=128)
                    P.add('pool', lambda e, dst=dst, srcv=srcv: e.dma_start(out=dst, in_=srcv),
                          writes=[('wo', g % 2, i)], dma_key=f'wo{g % 2}_{i}')

            def out_proj(g):
                fa, fb = groups[g]
                for t in range(NT):
                    cols = slice(t * 512, (t + 1) * 512)
                    for d in range(KD):
                        po, pok = ps_next()

                        def fn(e, d=d, cols=cols, po=po):
                            ins = None
                            for f in range(fa, fb):
                                ins = e.matmul(po[:, :], lhsT=wo_buf[g % 2][:, f - fa, d * 128:(d + 1) * 128], rhs=B_bf[:, at_slot[f], cols],
                                               start=(f == fa), stop=(f == fb - 1))
                            return ins
                        P.add('pe', fn, reads=[('wo', g % 2, i) for i in range((fb - fa) // 2)] + [('aT', at_slot[f], t) for f in range(fa, fb)],
                              writes=[pok])
                        P.add('dve', lambda e, d=d, cols=cols, po=po: e.scalar_tensor_tensor(out=xT[:, d, cols], in0=po[:, :], scalar=prm[:, ghcol + d:ghcol + d + 1],
                                                                                            in1=xT[:, d, cols], op0=ALU.mult, op1=ALU.add),
                              reads=[pok, ('prm', ghcol), ('xT', d, t)], writes=[('xT', d, t)])
                    if g == 3 and post_tile is not None:
                        post_tile(t)

            pairs_of = [list(range(a // 2, b // 2)) for a, b in groups]

            def pair_and_extra(pr):
                in_pair(pr)
                if extra:
                    extra.pop(0)()
            for g in range(4):
                prs = pairs_of[g]
                if g == 0:
                    pair_and_extra(prs[0])
                load_wout(g)
                for pr in prs[1:]:
                    pair_and_extra(pr)
                if g + 1 < 4:
                    pair_and_extra(pairs_of[g + 1][0])
                if g == 0 and pre_out0 is not None:
                    pre_out0()
                if g == 3:
                    while extra:
                        extra.pop(0)()
                out_proj(g)
            while extra:
                extra.pop(0)()

        yT = A_bf

        def cw(kk, j):
            return vecs[:, V_CW + kk * 4 + j: V_CW + kk * 4 + j + 1]

        E1, E1K = TMP[:, 4, :], ('tmp', 4)
        E2, E2K = TMP[:, 5, :], ('tmp', 5)
        D_hTt0 = hTt
        fl = vecs[:, V_FLAG:V_FLAG + 1]
        PREFIX_OFF = {'m231', 'm3', 'm4s', 'm4c', 'm4o', 'm5'}

        gn = vecs[:, V_GN:V_GN + 1]

        def gate_norm(t):
            cols = slice(t * 512, (t + 1) * 512)
            oT = oTb[t % 2]
            for h in range(4):
                p, hh = h // 2, h % 2
                okeys = [('oT', hh, t % 2, c) for c in range(4)]
                b = h % 2
                P.add('act', lambda e, h=h, b=b, cols=cols: e.activation(out=sqt[:, b, :], in_=oT[:, h, :], func=AF.Square),
                      reads=okeys, writes=[('sqt', b)])
                ps2, ps2k = ps_next()
                P.add('pe', lambda e, ps2=ps2, b=b: e.matmul(ps2[:, :], lhsT=ones_bf[:, :], rhs=sqt[:, b, :], start=True, stop=True),
                      reads=[('sqt', b), 'ones'], writes=[ps2k])
                rs, rsk = tmp_next()
                P.add('act', lambda e, rs=rs, ps2=ps2: e.activation(out=rs, in_=ps2[:, :], func=AF.Ln, scale=1.0 / 128, bias=prm[:, P_EPS:P_EPS + 1]),
                      reads=[ps2k, 'eps'], writes=[rsk])
                P.add('act', lambda e, rs=rs: e.activation(out=rs, in_=rs, func=AF.Exp, scale=-0.5), reads=[rsk], writes=[rsk])
                P.add('dve', lambda e, rs=rs, h=h, cols=cols: e.scalar_tensor_tensor(out=rs, in0=oT[:, h, :], scalar=gn, in1=rs, op0=ALU.mult, op1=ALU.mult),
                      reads=[rsk, 'vecs'] + okeys, writes=[rsk])
                P.add('dve', lambda e, rs=rs, h=h, cols=cols: e.tensor_tensor(out=yT[:, 4 + h, cols], in0=rs, in1=yT[:, 4 + h, cols], op=ALU.mult),
                      reads=[rsk, ('yT', 4 + h, t)], writes=[('yT', 4 + h, t)])

        hbufs = [hTt, hTt1]

        def norm_for(t):
            hb = hbufs[t % 2]
            norm_tile(t, lambda k: hb[:, k, :], lambda k: ('hTt', t % 2, k), P_A2, 3, ada_keys(3))

        def mixer_tiles(prefix, skip_norm0=False, last_hook=None):
            P.off = PREFIX_OFF if prefix else set()
            P.stage('mix_tiles')
            P.fence()
            ws_state['nslots'] = 3
            if prefix:
                P.add('pool', lambda e: e.dma_start(out=wgk[:, :, :], in_=wmi_d.rearrange("(k p) c -> p k c", p=128)[:, :, 3072:3088]),
                      writes=['wgk'], dma_key='wgk')
            P.add('dve', lambda e: e.memset(gk_aug[:, :], 1.0), writes=['gkaug'])
            skeys = [('S', 0), ('S', 1), ('Sbf', 0, 0), ('Sbf', 0, 1), ('Sbf', 1, 0), ('Sbf', 1, 1)]
            if prefix:
                P.add('dve', lambda e: e.memset(S_sb[:, :, :], 0.0), writes=[('S', 0), ('S', 1)])
                P.add('dve', lambda e: e.memset(S_bf[:, :, :], 0.0), writes=skeys[2:])
                P.add('dve', lambda e: e.memset(prm[:, P_UH:P_UH + 8], 0.0), writes=['uhalo'])
            else:
                P.add('dve', lambda e: e.memset(S_bf[:, 2:4, :], 0.0), writes=[('Sbf', 1, 0), ('Sbf', 1, 1)])
                for p in range(2):
                    P.add('dve', lambda e, p=p: e.tensor_scalar(out=S_sb[:, p, :], in0=spst[:, p * 256:(p + 1) * 256], scalar1=fl, scalar2=None, op0=ALU.mult),
                          reads=['spstA', 'vecs'], writes=[('S', p)])
                    P.add('dve', lambda e, p=p: e.tensor_scalar(out=S_bf[:, p, :], in0=spst[:, p * 256:(p + 1) * 256], scalar1=fl, scalar2=None, op0=ALU.mult),
                          reads=['spstA', 'vecs'], writes=[('Sbf', 0, p)])
                P.add('dve', lambda e: e.tensor_scalar(out=prm[:, P_UH:P_UH + 8], in0=spst[:, 512:520], scalar1=fl, scalar2=None, op0=ALU.mult),
                      reads=['spstB', 'vecs'], writes=['uhalo'])
            def tile_body(t):
                cols = slice(t * 512, (t + 1) * 512)
                hTt = hbufs[t % 2]
                hkeys = [('hTt', t % 2, k) for k in range(KD)]
                oT = oTb[t % 2]
                P.stage('m1')
                pgk_, pgkk = ps_next()

                def fn(e, pgk_=pgk_):
                    ins = None
                    for k in range(KD):
                        ins = e.matmul(pgk_[0:16, :], lhsT=wgk[:, k, :], rhs=hTt[:, k, :], start=(k == 0), stop=(k == KD - 1))
                    return ins
                P.add('pe', fn, reads=['wgk'] + hkeys, writes=[pgkk])
                P.add('act', lambda e, pgk_=pgk_: e.activation(out=gk_aug[0:16, :], in_=pgk_[0:16, :], func=AF.Identity), reads=[pgkk, 'gkaug'], writes=['gkaug'])
                s_qk = ws_alloc()
                if not prefix:
                    load_cols(s_qk, 0, wmi_d, 1536, 256)
                load_cols(s_qk, 1, wmi_d, 1792, 256)
                s_v = ws_alloc()
                load_cols(s_v, 0, wmi_d, 2048, 256)
                load_cols(s_v, 1, wmi_d, 2304, 256)
                for c in range(4):
                    gc = t * 4 + c
                    cc = slice(c * 128, (c + 1) * 128)
                    P.stage('m21')
                    pz, pzk = ps_next()
                    P.add('pe', lambda e, pz=pz, cc=cc: e.matmul(pz[:, 0:256], lhsT=gk_aug[0:17, cc], rhs=wgk2[0:17, :], start=True, stop=True),
                          reads=['gkaug', 'wgk2'], writes=[pzk])
                    sp_sb, spk = sp_bufs[gc % NB], ('sp', gc % NB)
                    er_sb, erk = er_bufs[gc % NB], ('er', gc % NB)
                    P.add('act', lambda e, pz=pz, sp_sb=sp_sb: e.activation(out=sp_sb, in_=pz[:, 0:256], func=AF.Exp, scale=-1.0), reads=[pzk], writes=[spk])
                    P.add('act', lambda e, sp_sb=sp_sb: e.activation(out=sp_sb, in_=sp_sb, func=AF.Ln, bias=1.0, scale=1.0), reads=[spk], writes=[spk])
                    P.stage('m22')
                    pb, pbk = ps_next()

                    def fn(e, pb=pb, sp_sb=sp_sb):
                        e.matmul(pb[:, 0:128], lhsT=sp_sb[:, 0:128], rhs=tris[:, :], start=True, stop=True)
                        e.matmul(pb[:, 128:256], lhsT=sp_sb[:, 128:256], rhs=tris[:, :], start=True, stop=True)
                        return e.matmul(pb[:, 256:512], lhsT=triu[:, :], rhs=sp_sb[:, :], start=True, stop=True)
                    P.add('pe', fn, reads=[spk, 'tris', 'triu'], writes=[pbk])
                    pb3 = pb[:, 0:256].rearrange("p (a t) -> p a t", a=2)
                    P.stage('m231')
                    P.add('act', lambda e, pb3=pb3, cc=cc: e.activation(out=bT_sb[:, :, cc], in_=pb3, func=AF.Identity), reads=[pbk], writes=[('bT', c)])
                    P.stage('m233')
                    P.add('act', lambda e, pb3=pb3, c=c: e.activation(out=small[:, 2 * c:2 * c + 2].rearrange("p (a o) -> p a o", o=1), in_=pb3[:, :, 127:128], func=AF.Exp),
                          reads=[pbk], writes=[('ebl', c)])
                    P.stage('m234')
                    P.add('act', lambda e, pb=pb, er_sb=er_sb: e.activation(out=er_sb, in_=pb[:, 256:512], func=AF.Exp), reads=[pbk], writes=[erk])
                    P.stage('m24')
                    pk, pkk = ps_next()

                    def fn(e, pk=pk, cc=cc, s_qk=s_qk):
                        ins = None
                        for k in range(KD):
                            ins = e.matmul(pk[:, 0:256], lhsT=hTt[:, k, cc], rhs=wslot(s_qk)[:, k, 256:512], start=(k == 0), stop=(k == KD - 1))
                        return ins
                    P.add('pe', fn, reads=wsk(s_qk, [2, 3]) + hkeys, writes=[pkk])
                    P.add('dve', lambda e, pk=pk, c=c, er_sb=er_sb: e.tensor_tensor(out=khat[:, c, :], in0=pk[:, 0:256], in1=er_sb, op=ALU.mult),
                          reads=[pkk, erk], writes=[('khat', c)])
                    P.stage('m25')
                    pv, pvk = ps_next()

                    def fn(e, pv=pv, cc=cc, s_v=s_v):
                        ins = None
                        for k in range(KD):
                            ins = e.matmul(pv[:, :], lhsT=hTt[:, k, cc], rhs=wslot(s_v)[:, k, :], start=(k == 0), stop=(k == KD - 1))
                        return ins
                    P.add('pe', fn, reads=wsk(s_v, range(4)) + hkeys, writes=[pvk])
                    P.add('act', lambda e, pv=pv, c=c: e.activation(out=vtok[:, c, :], in_=pv[:, :], func=AF.Identity), reads=[pvk], writes=[('vtok', c)])
                P.stage('m3')
                bkeys = [('bT', c) for c in range(4)]
                for p in range(2):
                    pq, pqk = ps_next()

                    def fn(e, pq=pq, p=p, s_qk=s_qk):
                        ins = None
                        for k in range(KD):
                            ins = e.matmul(pq[:, :], lhsT=wslot(s_qk)[:, k, p * 128:(p + 1) * 128], rhs=hTt[:, k, :], start=(k == 0), stop=(k == KD - 1))
                        return ins
                    P.add('pe', fn, reads=wsk(s_qk, [p]) + hkeys, writes=[pqk])
                    pk2, pk2k = ps_next()

                    def fn(e, pk2=pk2, p=p, s_qk=s_qk):
                        ins = None
                        for k in range(KD):
                            ins = e.matmul(pk2[:, :], lhsT=wslot(s_qk)[:, k, 256 + p * 128:256 + (p + 1) * 128], rhs=hTt[:, k, :], start=(k == 0), stop=(k == KD - 1))
                        return ins
                    P.add('pe', fn, reads=wsk(s_qk, [2 + p]) + hkeys, writes=[pk2k])
                    P.add('act', lambda e, p=p: e.activation(out=E1, in_=bT_sb[:, p, :], func=AF.Exp), reads=bkeys, writes=[E1K])
                    P.add('dve', lambda e, p=p, pq=pq: e.scalar_tensor_tensor(out=qtT[:, p, :], in0=pq[:, :], scalar=0.125, in1=E1, op0=ALU.mult, op1=ALU.mult),
                          reads=[pqk, E1K], writes=[('qt', p)])
                    P.add('act', lambda e, p=p: e.activation(out=E1, in_=bT_sb[:, p, :], func=AF.Exp, scale=-1.0), reads=bkeys, writes=[E1K])
                    P.add('dve', lambda e, p=p, pk2=pk2: e.tensor_tensor(out=ktT[:, p, :], in0=pk2[:, :], in1=E1, op=ALU.mult),
                          reads=[pk2k, E1K], writes=[('kt', p)])
                if t > 0:
                    P.stage('m5')
                    gate_norm(t - 1)
                if t + 1 < NT:
                    P.stage('m1')
                    norm_for(t + 1)
                elif last_hook is not None:
                    last_hook()
                for c in range(4):
                    gc = t * 4 + c
                    cc = slice(c * 128, (c + 1) * 128)
                    gcols = slice(t * 512 + c * 128, t * 512 + (c + 1) * 128)
                    P.stage('m4s')
                    for hh in range(2):
                        pscb, psck = ps_next()

                        def fn(e, hh=hh, pscb=pscb, cc=cc):
                            ins = None
                            for p in range(2):
                                ins = e.matmul(pscb[:, p * 128:(p + 1) * 128], lhsT=ktT[hh * 64:(hh + 1) * 64, p, cc], rhs=qtT[hh * 64:(hh + 1) * 64, p, cc],
                                               start=True, stop=True)
                            return ins
                        P.add('pe', fn, reads=[('kt', 0), ('kt', 1), ('qt', 0), ('qt', 1)], writes=[psck])
                        P.add('dve', lambda e, hh=hh, pscb=pscb, c=c: e.tensor_tensor(out=scm[:, (c % 2) * 2 + hh, :], in0=pscb[:, 0:256], in1=mask[:, :], op=ALU.mult),
                              reads=[psck, 'mask'], writes=[('scm', c % 2, hh)])
                    P.stage('m4c')
                    j = c
                    s = ws_alloc()
                    load_cols(s, 0, wmi_d, j * 128, 128, off=0)
                    load_cols(s, 0, wmi_d, 512 + j * 128, 128, off=128)
                    load_cols(s, 1, wmi_d, 1024 + j * 128, 128, off=0)
                    load_cols(s, 1, wmi_d, 2560 + j * 128, 128, off=128)
                    pp = [ps_next() for _ in range(4)]
                    for i4 in range(4):
                        pbank, pkey = pp[i4]

                        def fn(e, pbank=pbank, i4=i4, s=s):
                            ins = None
                            for k in range(KD):
                                ins = e.matmul(pbank[:, :], lhsT=wslot(s)[:, k, i4 * 128:(i4 + 1) * 128], rhs=hTt[:, k, :], start=(k == 0), stop=(k == KD - 1))
                            return ins
                        P.add('pe', fn, reads=wsk(s, [i4]) + hkeys, writes=[pkey])
                    (pcb, pcbk), (pcc, pcck), (pcv, pcvk), (pg_, pgk2) = pp
                    P.add('act', lambda e, pcc=pcc: e.activation(out=E1, in_=pcc[:, :], func=AF.Identity), reads=[pcck], writes=[E1K])
                    P.add('act', lambda e, j=j: e.activation(out=ubuf[:, 0:2], in_=prm[:, P_UH + 2 * j:P_UH + 2 * j + 2], func=AF.Identity), reads=['uhalo'], writes=['ubuf'])
                    P.add('dve', lambda e, pcv=pcv: e.tensor_tensor(out=ubuf[:, 2:514], in0=E1, in1=pcv[:, :], op=ALU.mult), reads=[E1K, pcvk, 'ubuf'], writes=['ubuf'])
                    P.add('act', lambda e, j=j: e.activation(out=E1, in_=ubuf[:, 2:514], func=AF.Identity, scale=cw(2, j)), reads=['ubuf', 'vecs'], writes=[E1K])
                    P.add('dve', lambda e, j=j: e.scalar_tensor_tensor(out=E2, in0=ubuf[:, 1:513], scalar=cw(1, j), in1=E1, op0=ALU.mult, op1=ALU.add),
                          reads=['ubuf', E1K, 'vecs'], writes=[E2K])
                    P.add('dve', lambda e, j=j: e.scalar_tensor_tensor(out=E1, in0=ubuf[:, 0:512], scalar=cw(0, j), in1=E2, op0=ALU.mult, op1=ALU.add),
                          reads=['ubuf', E2K, 'vecs'], writes=[E1K])
                    P.add('dve', lambda e, j=j, pcb=pcb, cols=cols: e.tensor_tensor(out=yT[:, j, cols], in0=pcb[:, :], in1=E1, op=ALU.mult), reads=[pcbk, E1K], writes=[('yT', j, t)])
                    P.add('act', lambda e, j=j: e.activation(out=prm[:, P_UH + 2 * j:P_UH + 2 * j + 2], in_=ubuf[:, 512:514], func=AF.Identity), reads=['ubuf'], writes=['uhalo'])
                    P.add('act', lambda e, j=j, pg_=pg_, cols=cols: e.activation(out=yT[:, 4 + j, cols], in_=pg_[:, :], func=AF.Silu), reads=[pgk2], writes=[('yT', 4 + j, t)])
                    P.stage('m4o')
                    sbuf_i = gc % 2
                    for hh in range(2):
                        po, pok = ps_next()

                        def fn(e, hh=hh, po=po, c=c, cc=cc, sbuf_i=sbuf_i):
                            ins = None
                            for p in range(2):
                                h = 2 * p + hh
                                e.matmul(po[:, p * 128:(p + 1) * 128], lhsT=vtok[:, c, h * 128:(h + 1) * 128], rhs=scm[:, (c % 2) * 2 + hh, p * 128:(p + 1) * 128],
                                         start=True, stop=False)
                                ins = e.matmul(po[:, p * 128:(p + 1) * 128], lhsT=S_bf[hh * 64:(hh + 1) * 64, sbuf_i * 2 + p, hh * 128:(hh + 1) * 128],
                                               rhs=qtT[hh * 64:(hh + 1) * 64, p, cc], start=False, stop=True)
                            return ins
                        P.add('pe', fn, reads=[('vtok', c), ('scm', c % 2, hh), ('Sbf', sbuf_i, 0), ('Sbf', sbuf_i, 1), ('qt', 0), ('qt', 1)], writes=[pok])
                        P.add('act', lambda e, hh=hh, po=po, cc=cc: e.activation(out=oT[:, hh:4:2, cc], in_=po[:, 0:256].rearrange("p (a t) -> p a t", a=2), func=AF.Identity),
                              reads=[pok], writes=[('oT', hh, t % 2, c)])
                    P.stage('m4u')
                    pu_, puk_ = ps_next()

                    def fn(e, pu_=pu_, c=c):
                        e.matmul(pu_[:, 0:256], lhsT=khat[:, c, 0:128], rhs=vtok[:, c, 0:256], start=True, stop=True)
                        return e.matmul(pu_[:, 256:512], lhsT=khat[:, c, 128:256], rhs=vtok[:, c, 256:512], start=True, stop=True)
                    P.add('pe', fn, reads=[('khat', c), ('vtok', c)], writes=[puk_])
                    for p in range(2):
                        P.add('dve', lambda e, p=p, pu_=pu_, c=c: e.scalar_tensor_tensor(out=S_sb[:, p, :], in0=S_sb[:, p, :], scalar=small[:, 2 * c + p:2 * c + p + 1],
                                                                                         in1=pu_[:, p * 256:(p + 1) * 256], op0=ALU.mult, op1=ALU.add),
                              reads=[('S', p), puk_, ('ebl', c)], writes=[('S', p)])
                        P.add('act', lambda e, p=p, sbuf_i=sbuf_i: e.activation(out=S_bf[:, (1 - sbuf_i) * 2 + p, :], in_=S_sb[:, p, :], func=AF.Identity),
                              reads=[('S', p)], writes=[('Sbf', 1 - sbuf_i, p)])
            if not skip_norm0:
                P.stage('m1')
                norm_for(0)
            for t in range(NT):
                tile_body(t)
            P.stage('m5')
            gate_norm(NT - 1)
            P.off = set()
            P.stage('mix_tiles')
            if prefix:
                ws_state['nslots'] = 2

        def prefix_tail():
            hTt = [D_hTt0, hTt1][(NT - 1) % 2]
            hkeys = [('hTt', (NT - 1) % 2, k) for k in range(KD)]
            for j in range(4):
                s = ws_alloc()
                load_cols(s, 0, wmi_d, 512 + j * 128, 128, off=0)
                load_cols(s, 0, wmi_d, 1024 + j * 128, 128, off=128)
                pc_, pck_ = ps_next()

                def fn(e, s=s, pc_=pc_):
                    ins = None
                    for i2 in range(2):
                        for k in range(KD):
                            ins = e.matmul(pc_[:, 2 * i2:2 * i2 + 2], lhsT=wslot(s)[:, k, i2 * 128:(i2 + 1) * 128], rhs=hTt[:, k, 510:512],
                                           start=(k == 0), stop=(k == KD - 1))
                    return ins
                P.add('pe', fn, reads=wsk(s, [0, 1]) + hkeys, writes=[pck_])
                P.add('act', lambda e, pc_=pc_, j=j: e.activation(out=small[:, 48 + 2 * j:50 + 2 * j], in_=pc_[:, 0:2], func=AF.Identity), reads=[pck_], writes=[('ut', j)])
                P.add('dve', lambda e, pc_=pc_, j=j: e.tensor_tensor(out=spst[:, 512 + 2 * j:514 + 2 * j], in0=small[:, 48 + 2 * j:50 + 2 * j], in1=pc_[:, 2:4], op=ALU.mult),
                      reads=[pck_, ('ut', j)], writes=['spstB'])
            P.add('act', lambda e: e.activation(out=spst[:, 0:512].rearrange("p (a t) -> p a t", a=2), in_=S_sb[:, :, :], func=AF.Identity),
                  reads=[('S', 0), ('S', 1)], writes=['spstA'])

        P.stage('ada0')
        for _ in range(4):
            ada_block()
        make_A(P_A1, 1, V_N1)
        P.stage('pre_ffn1')
        first_norm = lambda t: norm_for(0) if t == 0 else None
        ffn(w1i_d, w1o_d, P_A1, 0, P_GH1, [ada2] * 12, lambda: make_half(P_GH1, 2), post_tile=lambda t: (make_A(P_A2, 4, V_NM), norm_for(0)) if t == 0 else None)

        def load_x_and_norm1():
            P.stage('ffn1')
            for k in range(KD):
                P.add('sp', lambda e, k=k: e.dma_start(out=xT[:, k, :], in_=xT_d[k * 128:(k + 1) * 128, :]),
                      writes=[('xT', k, t) for t in range(NT)], dma_key=f'x{k}')
            for t in range(NT):
                ffn_norm(t, P_A1, 0)
        mixer_tiles(True, skip_norm0=True, last_hook=load_x_and_norm1)
        P.stage('pre_tail')
        prefix_tail()
        P.stage('ffn1')
        P.fence()
        ffn(w1i_d, w1o_d, P_A1, 0, P_GH1, [ada2] * 12, None, do_norm=False, post_tile=first_norm)
        mixer_tiles(False, skip_norm0=True)

        P.stage('mixout')
        make_A(P_A3, 7, V_N2)
        wm_slots = [ws_alloc(), ws_alloc()]
        for i, s in enumerate(wm_slots):
            dst = wslot(s)[:, :, :].rearrange("p k c -> p (k c)").rearrange("p (k d) -> p k d", k=4)
            srcv = wmo_d[i * 512:(i + 1) * 512, :].rearrange("(k p) d -> p k d", p=128)
            P.add('pool', lambda e, dst=dst, srcv=srcv: e.dma_start(out=dst, in_=srcv), writes=wsk(s, range(4)), dma_key=f'wm{s}')
        wm_view = [wslot(s)[:, :, :].rearrange("p k c -> p (k c)").rearrange("p (k d) -> p k d", k=4) for s in wm_slots]
        for t in range(NT):
            cols = slice(t * 512, (t + 1) * 512)
            for d in range(KD):
                po, pok = ps_next()

                def fn(e, d=d, cols=cols, po=po):
                    ins = None
                    for kc in range(8):
                        ins = e.matmul(po[:, :], lhsT=wm_view[kc // 4][:, kc % 4, d * 128:(d + 1) * 128], rhs=yT[:, kc, cols], start=(kc == 0), stop=(kc == 7))
                    return ins
                P.add('pe', fn, reads=[('ws', s, q) for s in wm_slots for q in range(4)] + [('yT', kc, t) for kc in range(8)], writes=[pok])
                P.add('dve', lambda e, d=d, cols=cols, po=po: e.scalar_tensor_tensor(out=xT[:, d, cols], in0=po[:, :], scalar=ada[:, 40 + d:41 + d],
                                                                                    in1=xT[:, d, cols], op0=ALU.mult, op1=ALU.add),
                      reads=[pok, ('xT', d, t)] + ada_keys(5), writes=[('xT', d, t)])
            ffn_norm(t, P_A3, 6)

        def final_tile(t):
            cols = slice(t * 512, (t + 1) * 512)
            ps, psk = ps_next()
            for k in range(KD):
                b = k % 2
                P.add('act', lambda e, k=k, b=b, cols=cols: e.activation(out=sqt[:, b, :], in_=xT[:, k, cols], func=AF.Square),
                      reads=[('xT', k, t)], writes=[('sqt', b)])
                P.add('pe', lambda e, k=k, b=b, ps=ps: e.matmul(ps[:, :], lhsT=ones_bf[:, :], rhs=sqt[:, b, :], start=(k == 0), stop=(k == KD - 1)),
                      reads=[('sqt', b), 'ones'], writes=[psk])
            rs, rsk = tmp_next()
            P.add('act', lambda e, rs=rs, ps=ps: e.activation(out=rs, in_=ps[:, :], func=AF.Ln, scale=1.0 / D, bias=prm[:, P_EPS:P_EPS + 1]),
                  reads=[psk, 'eps'], writes=[rsk])
            P.add('act', lambda e, rs=rs: e.activation(out=rs, in_=rs, func=AF.Exp, scale=-0.5), reads=[rsk], writes=[rsk])
            for k in range(KD):
                P.add('dve', lambda e, k=k, rs=rs, cols=cols: e.scalar_tensor_tensor(out=xT[:, k, cols], in0=xT[:, k, cols], scalar=vecs[:, V_NF + k:V_NF + k + 1],
                                                                                    in1=rs, op0=ALU.mult, op1=ALU.mult),
                      reads=[('xT', k, t), rsk, 'vecs'], writes=[('xT', k, t)])
                P.add('sp', lambda e, k=k, cols=cols: e.dma_start(out=out_d[k * 128:(k + 1) * 128, cols], in_=xT[:, k, cols]),
                      reads=[('xT', k, t)], writes=[('out', k, t)], dma_key=f'o{k}')

        P.stage('ffn2')
        P.fence()
        ws_state['nslots'] = 2
        ffn(w2i_d, w2o_d, P_A3, 6, P_GH3, [ada2] * 4, lambda: make_half(P_GH3, 8), do_norm=False, post_tile=final_tile)
        P.emit(nc, st)
    return nc


_CACHE = {}


def kernel(x, c, w_ada, b_ada, norm_ffn1, w_ffn1_in, w_ffn1_out, norm_mix, w_mix_in, conv_w, w_gk2, b_gk,
           gla_norm, w_mix_out, norm_ffn2, w_ffn2_in, w_ffn2_out, norm_final):
    f = lambda a: np.ascontiguousarray(np.asarray(a, dtype=np.float32))
    x = f(x)
    B, T, _ = x.shape
    TT = T // 2
    if T not in _CACHE:
        _CACHE[T] = build(T)
    nc = _CACHE[T]

    def fm(v):
        v = f(v).reshape(-1, 128)
        return v.T

    jj, ii = np.meshgrid(np.arange(128), np.arange(128), indexing='ij')
    tris = np.where(jj <= ii, -1.0 / 16.0, 0.0).astype(np.float32)
    triu = np.where(jj > ii, -1.0 / 16.0, 0.0).astype(np.float32)
    mask = np.tile(np.where(jj <= ii, 1.0, 0.0).astype(np.float32), (1, 2))
    wgk2a = np.concatenate([f(w_gk2)[0], f(b_gk)[0][None, :]], axis=0)
    shared = {
        "w_ada": f(w_ada)[0], "w_ffn1_in": f(w_ffn1_in)[0], "w_ffn1_out": f(w_ffn1_out)[0],
        "w_mix_in": f(w_mix_in)[0], "wgk2a": f(wgk2a), "w_mix_out": f(w_mix_out)[0],
        "w_ffn2_in": f(w_ffn2_in)[0], "w_ffn2_out": f(w_ffn2_out)[0],
        "tris": tris, "triu": triu, "mask": mask,
    }
    cwv = f(conv_w)[0]
    in_maps = []
    for core in range(NCORES):
        b, half = core // 2, core % 2
        vecs = np.zeros((128, NV), np.float32)
        vecs[:, 0:8] = fm(f(c)[b])
        vecs[:, 8:80] = fm(f(b_ada)[0])
        vecs[:, 80:88] = fm(f(norm_ffn1)[0])
        vecs[:, 88:96] = fm(f(norm_mix)[0])
        vecs[:, 96:104] = fm(f(norm_ffn2)[0])
        vecs[:, 104:112] = fm(f(norm_final))
        for kk in range(3):
            vecs[:, 112 + kk * 4:112 + kk * 4 + 4] = fm(cwv[kk])
        vecs[:, 124] = f(gla_norm)[0]
        vecs[:, 125] = float(half)
        m = dict(shared)
        m["xT"] = np.ascontiguousarray(x[b, half * TT:(half + 1) * TT, :].T)
        m["xTp"] = np.ascontiguousarray(x[b, 0:TT, :].T) if half == 1 else np.zeros((D, TT), np.float32)
        m["vecs"] = vecs
        in_maps.append(m)
    res = run_bass_kernel_spmd(nc, in_maps, core_ids=list(range(NCORES)))
    out = np.empty((B, T, D), np.float32)
    for core in range(NCORES):
        b, half = core // 2, core % 2
        out[b, half * TT:(half + 1) * TT, :] = res.results[core]["outT"].T
    return out
```

```python
from contextlib import ExitStack
import numpy as np
import concourse.bass as bass
import concourse.mybir as mybir
from concourse.bass_utils import run_bass_kernel_spmd

F32 = mybir.dt.float32
BF16 = mybir.dt.bfloat16
AF = mybir.ActivationFunctionType
ALU = mybir.AluOpType

D = 1024
KD = 8
DFF = 2816
NF = 22
BATCH = 4
SEQ = 4096
NCORES = 8
EPS = 1e-6
NV = 129
XW = 520

ENGS = ['pe', 'act', 'dve', 'pool', 'sp']
EPOCH = 30000

STAGES = None


class Op:
    __slots__ = ('eng', 'fn', 'deps', 'dma_key', 'signals', 'val')


class Prog:
    def __init__(self):
        self.streams = {e: [] for e in ENGS}
        self.last_w = {}
        self.readers = {}
        self.dma_count = {}
        self.nops = 0
        self.pending_fence = {}

    def fence(self):
        if not self.enabled:
            return
        last = [self.streams[e][-1] for e in ('pe', 'act', 'dve') if self.streams[e]]
        for e in ENGS:
            self.pending_fence[e] = list(last)

    enabled = True

    off = set()

    def stage(self, name):
        if name in self.off:
            self.enabled = False
            return
        self.enabled = STAGES is None or name in STAGES or (name[0] == 'm' and name[1:].isdigit() and 'mix_tiles' in STAGES and not any(
            x[0] == 'm' and x[1:].isdigit() for x in STAGES))

    def add(self, eng, fn, reads=(), writes=(), dma_key=None, nofence=False):
        if not self.enabled:
            return None
        op = Op()
        op.eng = eng
        op.fn = fn
        op.dma_key = dma_key
        op.signals = False
        op.val = 0
        psr = [r for r in reads if isinstance(r, tuple) and r[0] == 'ps']
        if psr:
            writes = list(writes) + psr
        deps = set()
        for r in reads:
            w = self.last_w.get(r)
            if w is not None:
                deps.add(w)
        for r in writes:
            w = self.last_w.get(r)
            if w is not None:
                deps.add(w)
            rd = self.readers.get(r)
            if rd:
                deps.update(rd.values())
        if eng == 'pe' and dma_key is None:
            deps = {d for d in deps if not (d.eng == 'pe' and d.dma_key is None)}
        if self.pending_fence.get(eng) and not nofence:
            for d in self.pending_fence[eng]:
                if not (eng == 'pe' and d.eng == 'pe'):
                    deps.add(d)
            self.pending_fence[eng] = None
        op.deps = deps
        for d in deps:
            d.signals = True
        for r in reads:
            rk = eng if dma_key is None else ('dma', self.nops)
            self.readers.setdefault(r, {})[rk] = op
        for r in writes:
            self.last_w[r] = op
            self.readers[r] = {}
        if dma_key is not None:
            self.dma_count[dma_key] = self.dma_count.get(dma_key, 0) + 16
            op.val = self.dma_count[dma_key]
        self.streams[eng].append(op)
        self.nops += 1
        return op

    def emit(self, nc, st):
        nep = {}
        for e in ENGS:
            cnt = 0
            for op in self.streams[e]:
                if op.dma_key is None and op.signals:
                    cnt += 1
                    op.val = cnt
            nep[e] = cnt // EPOCH + 1
        sems = {e: [st.enter_context(nc.semaphore(f"s_{e}{i}")) for i in range(nep[e])] for e in ENGS}
        dsems = {k: st.enter_context(nc.semaphore(f"d_{k}")) for k in self.dma_count}
        block = st.enter_context(nc.Block())

        def sem_of(d):
            if d.dma_key is not None:
                return ('d', d.dma_key), dsems[d.dma_key], d.val
            ep = (d.val - 1) // EPOCH
            return (d.eng, ep), sems[d.eng][ep], d.val - ep * EPOCH

        streams = self.streams
        dma_count = self.dma_count

        def mk(e):
            def body(engobj):
                waited = {}
                for op in streams[e]:
                    need = {}
                    for d in op.deps:
                        k, s, v = sem_of(d)
                        if k not in need or need[k][1] < v:
                            need[k] = (s, v)
                    for k, (s, v) in need.items():
                        if waited.get(k, 0) >= v:
                            continue
                        engobj.wait_ge(s, v)
                        waited[k] = v
                    ins = op.fn(engobj)
                    if op.dma_key is not None:
                        ins.then_inc(dsems[op.dma_key], 16)
                    elif op.signals:
                        k, s, v = sem_of(op)
                        ins.then_inc(s, 1)
                if e == 'sp':
                    for k, c in dma_count.items():
                        engobj.wait_ge(dsems[k], c)
            return body

        block.tensor(mk('pe'))
        block.scalar(mk('act'))
        block.vector(mk('dve'))
        block.gpsimd(mk('pool'))
        block.sync(mk('sp'))


def build(seq):
    TT = seq // 2
    NT = TT // 512
    nc = bass.Bass("TRN2", target_bir_lowering=False)

    def din(name, shape):
        return nc.dram_tensor(name, shape, F32, kind="ExternalInput").ap()

    xT_d = din("xT", [D, TT])
    xTp_d = din("xTp", [D, TT])
    vecs_d = din("vecs", [128, NV])
    wada_d = din("w_ada", [D, 9 * D])
    w1i_d = din("w_ffn1_in", [D, 2 * DFF])
    w1o_d = din("w_ffn1_out", [DFF, D])
    wmi_d = din("w_mix_in", [D, 3088])
    wgk2_d = din("wgk2a", [17, 256])
    wmo_d = din("w_mix_out", [D, D])
    w2i_d = din("w_ffn2_in", [D, 2 * DFF])
    w2o_d = din("w_ffn2_out", [DFF, D])
    tris_d = din("tris", [128, 128])
    triu_d = din("triu", [128, 128])
    mask_d = din("mask", [128, 256])
    out_d = nc.dram_tensor("outT", [D, TT], F32, kind="ExternalOutput").ap()

    P = Prog()
    with ExitStack() as st:
        def sb(name, shape, dt):
            return st.enter_context(nc.sbuf_tensor(name, shape, dt))

        xT = sb("xT_sb", [128, KD, TT], F32)
        A_raw = sb("arenaA", [128, 4 * TT], F32)
        B_raw = sb("arenaB", [128, max(4 * TT, 8192)], F32)
        D_raw = sb("arenaD", [128, 9760], F32)
        WA = sb("wada_slots", [128, 2, KD, 128], BF16)
        WS = sb("wslots", [128, 2, KD, 512], BF16)
        TMP = sb("tmp", [128, 6, 512], F32)
        vecs = sb("vecs_sb", [128, NV], F32)
        ada = sb("ada_sb", [128, 72], F32)
        prm = sb("prm_sb", [128, 64], F32)
        cact = sb("cact", [128, KD], BF16)
        ones_bf = sb("ones_bf", [128, 128], BF16)
        tris = sb("tris_sb", [128, 128], F32)
        triu = sb("triu_sb", [128, 128], F32)
        mask = sb("mask_sb", [128, 256], F32)
        wgk2 = sb("wgk2_sb", [32, 256], F32)
        wgk = sb("wgk_sb", [128, KD, 16], BF16)
        sqt = sb("sqt", [128, 2, 512], BF16)
        small = sb("small", [128, 64], F32)
        spst = sb("spst", [128, XW], F32)

        PS = [st.enter_context(nc.psum_tensor(f"ps{i}", [128, 512], F32)) for i in range(8)]
        ps_state = {'i': 0}

        def ps_next():
            i = ps_state['i'] % 8
            ps_state['i'] += 1
            return PS[i], ('ps', i)

        tmp_state = {'i': 0}

        def tmp_next():
            i = tmp_state['i'] % 4
            tmp_state['i'] += 1
            return TMP[:, i, :], ('tmp', i)

        A_bf = A_raw[:, :].bitcast(BF16).rearrange("p (k t) -> p k t", k=KD)
        B_bf = B_raw[:, 0:4 * TT].bitcast(BF16).rearrange("p (k t) -> p k t", k=KD)
        oTb = [B_raw[:, b * 2048:(b + 1) * 2048].rearrange("p (h t) -> p h t", h=4) for b in range(2)]
        hTt1 = B_raw[:, 4096:6144].bitcast(BF16).rearrange("p (k t) -> p k t", k=KD)
        WS3 = B_raw[:, 6144:8192].bitcast(BF16).rearrange("p (k c) -> p k c", k=KD)

        def wslot(s):
            return WS3 if s == 2 else WS[:, s]
        D_bf = D_raw[:, :].bitcast(BF16)
        wo_buf = [D_bf[:, i * 6144:(i + 1) * 6144].rearrange("p (f d) -> p f d", f=6) for i in range(2)]
        o0 = 0
        hTt = D_bf[:, o0:o0 + 4096].rearrange("p (k t) -> p k t", k=KD)
        o0 += 4096
        qtT = D_bf[:, o0:o0 + 1024].rearrange("p (a t) -> p a t", a=2)
        o0 += 1024
        ktT = D_bf[:, o0:o0 + 1024].rearrange("p (a t) -> p a t", a=2)
        o0 += 1024
        khat = D_bf[:, o0:o0 + 1024].rearrange("p (a t) -> p a t", a=4)
        o0 += 1024
        vtok = D_bf[:, o0:o0 + 2048].rearrange("p (a t) -> p a t", a=4)
        o0 += 2048
        scm = D_bf[:, o0:o0 + 1024].rearrange("p (a t) -> p a t", a=4)
        o0 += 1024
        S_bf = D_bf[:, o0:o0 + 1024].rearrange("p (a t) -> p a t", a=4)
        o0 += 1024
        f0 = o0 // 2
        bT_sb = D_raw[:, f0:f0 + 1024].rearrange("p (a t) -> p a t", a=2)
        f0 += 1024
        NB = 3
        sp_bufs = [D_raw[:, f0 + 256 * i:f0 + 256 * (i + 1)] for i in range(NB)]
        f0 += 256 * NB
        er_bufs = [D_raw[:, f0 + 256 * i:f0 + 256 * (i + 1)] for i in range(NB)]
        f0 += 256 * NB
        ubuf = D_raw[:, f0:f0 + 516]
        f0 += 516
        S_sb = D_raw[:, f0:f0 + 512].rearrange("p (a t) -> p a t", a=2)
        f0 += 512
        gk_aug = D_raw[0:32, f0:f0 + 512]
        f0 += 512
        assert f0 <= 9760, f0
        V_C, V_BADA, V_N1, V_NM, V_N2, V_NF, V_CW, V_GN, V_FLAG = 0, 8, 80, 88, 96, 104, 112, 124, 125
        P_A1, P_A2, P_A3, P_GH1, P_GH3, P_EPS = 0, 8, 16, 24, 32, 40
        P_CB01 = 44
        P_UH = 52

        P.add('sp', lambda e: e.dma_start(out=vecs[:, :], in_=vecs_d[:, :]), writes=['vecs'], dma_key='vecs')
        for k in range(KD):
            P.add('sp', lambda e, k=k: e.dma_start(out=xT[:, k, :], in_=xTp_d[k * 128:(k + 1) * 128, :]),
                  writes=[('xT', k, t) for t in range(NT)], dma_key=f'x{k}')
        P.add('sp', lambda e: e.dma_start(out=tris[:, :], in_=tris_d[:, :]), writes=['tris'], dma_key='c0')
        P.add('sp', lambda e: e.dma_start(out=triu[:, :], in_=triu_d[:, :]), writes=['triu'], dma_key='c1')
        P.add('sp', lambda e: e.dma_start(out=mask[:, :], in_=mask_d[:, :]), writes=['mask'], dma_key='c2')
        P.add('sp', lambda e: e.dma_start(out=wgk2[0:17, :], in_=wgk2_d[:, :]), writes=['wgk2'], dma_key='c3')
        P.add('dve', lambda e: e.memset(ones_bf[:, :], 1.0), writes=['ones'])
        P.add('dve', lambda e: e.memset(prm[:, P_EPS:P_EPS + 1], EPS), writes=['eps'])
        P.add('act', lambda e: e.activation(out=cact[:, :], in_=vecs[:, V_C:V_C + 8], func=AF.Silu),
              reads=['vecs'], writes=['cact'])

        ws_state = {'n': 0, 'nslots': 2}

        def ws_alloc():
            s = ws_state['n'] % ws_state['nslots']
            ws_state['n'] += 1
            return s

        def load_cols(slot, half, src, c0, ncols, off=0):
            co = half * 256 + off
            dst = wslot(slot)[:, :, co: co + ncols]
            srcv = src.rearrange("(k p) c -> p k c", p=128)[:, :, c0:c0 + ncols]
            qs = list(range(co // 128, (co + ncols) // 128))
            P.add('pool', lambda e: e.dma_start(out=dst, in_=srcv),
                  writes=[('ws', slot, q) for q in qs], dma_key=f'ws{slot}_{qs[0]}', nofence=(slot != 2))

        def wsk(slot, qs):
            return [('ws', slot, q) for q in qs]

        ada_state = {'blk': 0, 'c128': 16}
        wa_state = {'n': 0}

        def ada_block():
            blk = ada_state['blk']
            ada_state['blk'] += 1
            s = ws_alloc()
            load_cols(s, 0, wada_d, blk * 512, 256)
            load_cols(s, 1, wada_d, blk * 512 + 256, 256)
            pa, pak = ps_next()

            def fn(e, s=s, pa=pa):
                ins = None
                for cc in range(4):
                    for k in range(KD):
                        ins = e.matmul(pa[:, cc:cc + 1], lhsT=wslot(s)[:, k, cc * 128:(cc + 1) * 128], rhs=cact[:, k:k + 1],
                                       start=(k == 0), stop=(k == KD - 1))
                return ins
            P.add('pe', fn, reads=wsk(s, range(4)) + ['cact'], writes=[pak])
            P.add('dve', lambda e, blk=blk, pa=pa: e.tensor_tensor(out=ada[:, blk * 4:blk * 4 + 4], in0=pa[:, 0:4],
                                                                   in1=vecs[:, V_BADA + blk * 4:V_BADA + blk * 4 + 4], op=ALU.add),
                  reads=[pak, 'vecs'], writes=[('ada', blk * 4 + i) for i in range(4)])

        def ada_chunk():
            c = ada_state['c128']
            ada_state['c128'] += 1
            s = wa_state['n'] % 2
            wa_state['n'] += 1
            srcv = wada_d.rearrange("(k p) c -> p k c", p=128)[:, :, c * 128:(c + 1) * 128]
            P.add('pool', lambda e: e.dma_start(out=WA[:, s, :, :], in_=srcv), writes=[('wa', s)], dma_key=f'wa{s}')
            pa, pak = ps_next()

            def fn(e):
                ins = None
                for k in range(KD):
                    ins = e.matmul(pa[:, 0:1], lhsT=WA[:, s, k, :], rhs=cact[:, k:k + 1], start=(k == 0), stop=(k == KD - 1))
                return ins
            P.add('pe', fn, reads=[('wa', s), 'cact'], writes=[pak])
            P.add('dve', lambda e: e.tensor_tensor(out=ada[:, c:c + 1], in0=pa[:, 0:1], in1=vecs[:, V_BADA + c:V_BADA + c + 1], op=ALU.add),
                  reads=[pak, 'vecs'], writes=[('ada', c)])

        def ada2():
            ada_chunk()
            ada_chunk()

        def ada_keys(n):
            return [('ada', 8 * n + i) for i in range(8)]

        def make_A(col, n_sc, vcol):
            P.add('dve', lambda e: e.scalar_tensor_tensor(out=prm[:, col:col + 8], in0=ada[:, n_sc * 8:n_sc * 8 + 8], scalar=1.0,
                                                          in1=vecs[:, vcol:vcol + 8], op0=ALU.add, op1=ALU.mult),
                  reads=ada_keys(n_sc) + ['vecs'], writes=[('prm', col)])

        def make_half(col, n_g):
            P.add('dve', lambda e: e.tensor_scalar(out=prm[:, col:col + 8], in0=ada[:, n_g * 8:n_g * 8 + 8], scalar1=0.5, scalar2=None,
                                                   op0=ALU.mult),
                  reads=ada_keys(n_g), writes=[('prm', col)])

        def norm_tile(t, dst, dst_key, acol, sh_n, sh_reads):
            cols = slice(t * 512, (t + 1) * 512)
            ps, psk = ps_next()
            for k in range(KD):
                b = k % 2
                P.add('act', lambda e, k=k, b=b: e.activation(out=sqt[:, b, :], in_=xT[:, k, cols], func=AF.Square),
                      reads=[('xT', k, t)], writes=[('sqt', b)])
                P.add('pe', lambda e, k=k, b=b: e.matmul(ps[:, :], lhsT=ones_bf[:, :], rhs=sqt[:, b, :],
                                                         start=(k == 0), stop=(k == KD - 1)),
                      reads=[('sqt', b), 'ones'], writes=[psk])
            rs, rsk = TMP[:, 4, :], ('tmp', 4)
            P.add('act', lambda e: e.activation(out=rs, in_=ps[:, :], func=AF.Ln, scale=1.0 / D, bias=prm[:, P_EPS:P_EPS + 1]),
                  reads=[psk, 'eps'], writes=[rsk])
            P.add('act', lambda e: e.activation(out=rs, in_=rs, func=AF.Exp, scale=-0.5), reads=[rsk], writes=[rsk])
            for k in range(KD):
                tm, tmk = tmp_next()
                P.add('dve', lambda e, k=k, tm=tm: e.scalar_tensor_tensor(out=tm, in0=xT[:, k, cols], scalar=prm[:, acol + k:acol + k + 1],
                                                                          in1=rs, op0=ALU.mult, op1=ALU.mult),
                      reads=[('xT', k, t), ('prm', acol), rsk], writes=[tmk])
                P.add('act', lambda e, k=k, tm=tm: e.activation(out=dst(k), in_=tm, func=AF.Identity,
                                                                bias=ada[:, sh_n * 8 + k:sh_n * 8 + k + 1], scale=1.0),
                      reads=[tmk] + sh_reads, writes=[dst_key(k)])

        def ffn_norm(t, acol, sh_n):
            norm_tile(t, lambda k, t=t: A_bf[:, k, t * 512:(t + 1) * 512], lambda k, t=t: ('hT', k, t),
                      acol, sh_n, ada_keys(sh_n))

        def ffn(win_d, wout_d, acol, sh_n, ghcol, extra, pre_out0, do_norm=True, post_tile=None):
            if do_norm:
                for t in range(NT):
                    ffn_norm(t, acol, sh_n)
            groups = [(0, 6), (6, 12), (12, 18), (18, 22)]
            slot_ctr = {'n': 0}
            at_slot = {}

            def in_pair(pr):
                s = ws_alloc()
                f0_ = pr * 2
                load_cols(s, 0, win_d, f0_ * 128, 256)
                load_cols(s, 1, win_d, DFF + f0_ * 128, 256)
                for fi in range(2):
                    f = f0_ + fi
                    sl = slot_ctr['n'] % 8
                    slot_ctr['n'] += 1
                    at_slot[f] = sl
                    for t in range(NT):
                        cols = slice(t * 512, (t + 1) * 512)
                        pg, pgk = ps_next()
                        pu, puk = ps_next()

                        def fn(e, s=s, fi=fi, cols=cols, pg=pg, pu=pu):
                            ins = None
                            for k in range(KD):
                                ins = e.matmul(pg[:, :], lhsT=wslot(s)[:, k, fi * 128:(fi + 1) * 128], rhs=A_bf[:, k, cols],
                                               start=(k == 0), stop=(k == KD - 1))
                            for k in range(KD):
                                ins = e.matmul(pu[:, :], lhsT=wslot(s)[:, k, 256 + fi * 128:256 + (fi + 1) * 128], rhs=A_bf[:, k, cols],
                                               start=(k == 0), stop=(k == KD - 1))
                            return ins
                        P.add('pe', fn, reads=wsk(s, [fi, 2 + fi]) + [('hT', k, t) for k in range(KD)], writes=[pgk, puk])
                        tm, tmk = tmp_next()
                        P.add('act', lambda e, tm=tm, pg=pg: e.activation(out=tm, in_=pg[:, :], func=AF.Silu), reads=[pgk], writes=[tmk])
                        P.add('dve', lambda e, tm=tm, pu=pu, sl=sl, cols=cols: e.tensor_tensor(out=B_bf[:, sl, cols], in0=tm, in1=pu[:, :], op=ALU.mult),
                              reads=[tmk, puk], writes=[('aT', sl, t)])

            def load_wout(g):
                fa, fb = groups[g]
                for i in range((fb - fa) // 2):
                    dst = wo_buf[g % 2][:, 2 * i:2 * i + 2, :]
                    srcv = wout_d[(fa + 2 * i) * 128:(fa + 2 * i + 2) * 128, :].rearrange("(f p) d -> p f d", p=128)
                    P.add('pool', lambda e, dst=dst, srcv=srcv: e.dma_start(out=dst, in_=srcv),
                          writes=[('wo', g % 2, i)], dma_key=f'wo{g % 2}_{i}')

            def out_proj(g):
                fa, fb = groups[g]
                for t in range(NT):
                    cols = slice(t * 512, (t + 1) * 512)
                    for d in range(KD):
                        po, pok = ps_next()

                        def fn(e, d=d, cols=cols, po=po):
                            ins = None
                            for f in range(fa, fb):
                                ins = e.matmul(po[:, :], lhsT=wo_buf[g % 2][:, f - fa, d * 128:(d + 1) * 128], rhs=B_bf[:, at_slot[f], cols],
                                               start=(f == fa), stop=(f == fb - 1))
                            return ins
                        P.add('pe', fn, reads=[('wo', g % 2, i) for i in range((fb - fa) // 2)] + [('aT', at_slot[f], t) for f in range(fa, fb)],
                              writes=[pok])
                        P.add('dve', lambda e, d=d, cols=cols, po=po: e.scalar_tensor_tensor(out=xT[:, d, cols], in0=po[:, :], scalar=prm[:, ghcol + d:ghcol + d + 1],
                                                                                            in1=xT[:, d, cols], op0=ALU.mult, op1=ALU.add),
                              reads=[pok, ('prm', ghcol), ('xT', d, t)], writes=[('xT', d, t)])
                    if g == 3 and post_tile is not None:
                        post_tile(t)

            pairs_of = [list(range(a // 2, b // 2)) for a, b in groups]

            def pair_and_extra(pr):
                in_pair(pr)
                if extra:
                    extra.pop(0)()
            for g in range(4):
                prs = pairs_of[g]
                if g == 0:
                    pair_and_extra(prs[0])
                load_wout(g)
                for pr in prs[1:]:
                    pair_and_extra(pr)
                if g + 1 < 4:
                    pair_and_extra(pairs_of[g + 1][0])
                if g == 0 and pre_out0 is not None:
                    pre_out0()
                if g == 3:
                    while extra:
                        extra.pop(0)()
                out_proj(g)
            while extra:
                extra.pop(0)()

        yT = A_bf

        def cw(kk, j):
            return vecs[:, V_CW + kk * 4 + j: V_CW + kk * 4 + j + 1]

        E1, E1K = TMP[:, 4, :], ('tmp', 4)
        E2, E2K = TMP[:, 5, :], ('tmp', 5)
        D_hTt0 = hTt
        fl = vecs[:, V_FLAG:V_FLAG + 1]
        PREFIX_OFF = {'m231', 'm3', 'm4s', 'm4c', 'm4o', 'm5'}

        gn = vecs[:, V_GN:V_GN + 1]

        def gate_norm(t):
            cols = slice(t * 512, (t + 1) * 512)
            oT = oTb[t % 2]
            for h in range(4):
                p, hh = h // 2, h % 2
                okeys = [('oT', hh, t % 2, c) for c in range(4)]
                b = h % 2
                P.add('act', lambda e, h=h, b=b, cols=cols: e.activation(out=sqt[:, b, :], in_=oT[:, h, :], func=AF.Square),
                      reads=okeys, writes=[('sqt', b)])
                ps2, ps2k = ps_next()
                P.add('pe', lambda e, ps2=ps2, b=b: e.matmul(ps2[:, :], lhsT=ones_bf[:, :], rhs=sqt[:, b, :], start=True, stop=True),
                      reads=[('sqt', b), 'ones'], writes=[ps2k])
                rs, rsk = tmp_next()
                P.add('act', lambda e, rs=rs, ps2=ps2: e.activation(out=rs, in_=ps2[:, :], func=AF.Ln, scale=1.0 / 128, bias=prm[:, P_EPS:P_EPS + 1]),
                      reads=[ps2k, 'eps'], writes=[rsk])
                P.add('act', lambda e, rs=rs: e.activation(out=rs, in_=rs, func=AF.Exp, scale=-0.5), reads=[rsk], writes=[rsk])
                P.add('dve', lambda e, rs=rs, h=h, cols=cols: e.scalar_tensor_tensor(out=rs, in0=oT[:, h, :], scalar=gn, in1=rs, op0=ALU.mult, op1=ALU.mult),
                      reads=[rsk, 'vecs'] + okeys, writes=[rsk])
                P.add('dve', lambda e, rs=rs, h=h, cols=cols: e.tensor_tensor(out=yT[:, 4 + h, cols], in0=rs, in1=yT[:, 4 + h, cols], op=ALU.mult),
                      reads=[rsk, ('yT', 4 + h, t)], writes=[('yT', 4 + h, t)])

        hbufs = [hTt, hTt1]

        def norm_for(t):
            hb = hbufs[t % 2]
            norm_tile(t, lambda k: hb[:, k, :], lambda k: ('hTt', t % 2, k), P_A2, 3, ada_keys(3))

        def mixer_tiles(prefix, skip_norm0=False, last_hook=None):
            P.off = PREFIX_OFF if prefix else set()
            P.stage('mix_tiles')
            P.fence()
            ws_state['nslots'] = 3
            if prefix:
                P.add('pool', lambda e: e.dma_start(out=wgk[:, :, :], in_=wmi_d.rearrange("(k p) c -> p k c", p=128)[:, :, 3072:3088]),
                      writes=['wgk'], dma_key='wgk')
            P.add('dve', lambda e: e.memset(gk_aug[:, :], 1.0), writes=['gkaug'])
            skeys = [('S', 0), ('S', 1), ('Sbf', 0, 0), ('Sbf', 0, 1), ('Sbf', 1, 0), ('Sbf', 1, 1)]
            if prefix:
                P.add('dve', lambda e: e.memset(S_sb[:, :, :], 0.0), writes=[('S', 0), ('S', 1)])
                P.add('dve', lambda e: e.memset(S_bf[:, :, :], 0.0), writes=skeys[2:])
                P.add('dve', lambda e: e.memset(prm[:, P_UH:P_UH + 8], 0.0), writes=['uhalo'])
            else:
                P.add('dve', lambda e: e.memset(S_bf[:, 2:4, :], 0.0), writes=[('Sbf', 1, 0), ('Sbf', 1, 1)])
                for p in range(2):
                    P.add('dve', lambda e, p=p: e.tensor_scalar(out=S_sb[:, p, :], in0=spst[:, p * 256:(p + 1) * 256], scalar1=fl, scalar2=None, op0=ALU.mult),
                          reads=['spstA', 'vecs'], writes=[('S', p)])
                    P.add('dve', lambda e, p=p: e.tensor_scalar(out=S_bf[:, p, :], in0=spst[:, p * 256:(p + 1) * 256], scalar1=fl, scalar2=None, op0=ALU.mult),
                          reads=['spstA', 'vecs'], writes=[('Sbf', 0, p)])
                P.add('dve', lambda e: e.tensor_scalar(out=prm[:, P_UH:P_UH + 8], in0=spst[:, 512:520], scalar1=fl, scalar2=None, op0=ALU.mult),
                      reads=['spstB', 'vecs'], writes=['uhalo'])
            def tile_body(t):
                cols = slice(t * 512, (t + 1) * 512)
                hTt = hbufs[t % 2]
                hkeys = [('hTt', t % 2, k) for k in range(KD)]
                oT = oTb[t % 2]
                P.stage('m1')
                pgk_, pgkk = ps_next()

                def fn(e, pgk_=pgk_):
                    ins = None
                    for k in range(KD):
                        ins = e.matmul(pgk_[0:16, :], lhsT=wgk[:, k, :], rhs=hTt[:, k, :], start=(k == 0), stop=(k == KD - 1))
                    return ins
                P.add('pe', fn, reads=['wgk'] + hkeys, writes=[pgkk])
                P.add('act', lambda e, pgk_=pgk_: e.activation(out=gk_aug[0:16, :], in_=pgk_[0:16, :], func=AF.Identity), reads=[pgkk, 'gkaug'], writes=['gkaug'])
                s_qk = ws_alloc()
                if not prefix:
                    load_cols(s_qk, 0, wmi_d, 1536, 256)
                load_cols(s_qk, 1, wmi_d, 1792, 256)
                s_v = ws_alloc()
                load_cols(s_v, 0, wmi_d, 2048, 256)
                load_cols(s_v, 1, wmi_d, 2304, 256)
                for c in range(4):
                    gc = t * 4 + c
                    cc = slice(c * 128, (c + 1) * 128)
                    P.stage('m21')
                    pz, pzk = ps_next()
                    P.add('pe', lambda e, pz=pz, cc=cc: e.matmul(pz[:, 0:256], lhsT=gk_aug[0:17, cc], rhs=wgk2[0:17, :], start=True, stop=True),
                          reads=['gkaug', 'wgk2'], writes=[pzk])
                    sp_sb, spk = sp_bufs[gc % NB], ('sp', gc % NB)
                    er_sb, erk = er_bufs[gc % NB], ('er', gc % NB)
                    P.add('act', lambda e, pz=pz, sp_sb=sp_sb: e.activation(out=sp_sb, in_=pz[:, 0:256], func=AF.Exp, scale=-1.0), reads=[pzk], writes=[spk])
                    P.add('act', lambda e, sp_sb=sp_sb: e.activation(out=sp_sb, in_=sp_sb, func=AF.Ln, bias=1.0, scale=1.0), reads=[spk], writes=[spk])
                    P.stage('m25')
                    pv, pvk = ps_next()

                    def fn(e, pv=pv, cc=cc, s_v=s_v):
                        ins = None
                        for k in range(KD):
                            ins = e.matmul(pv[:, :], lhsT=hTt[:, k, cc], rhs=wslot(s_v)[:, k, :], start=(k == 0), stop=(k == KD - 1))
                        return ins
                    P.add('pe', fn, reads=wsk(s_v, range(4)) + hkeys, writes=[pvk])
                    P.add('dve', lambda e, pv=pv, c=c: e.tensor_copy(out=vtok[:, c, :], in_=pv[:, :]), reads=[pvk], writes=[('vtok', c)])
                    P.stage('m22')
                    pb, pbk = ps_next()

                    def fn(e, pb=pb, sp_sb=sp_sb):
                        e.matmul(pb[:, 0:128], lhsT=sp_sb[:, 0:128], rhs=tris[:, :], start=True, stop=True)
                        e.matmul(pb[:, 128:256], lhsT=sp_sb[:, 128:256], rhs=tris[:, :], start=True, stop=True)
                        return e.matmul(pb[:, 256:512], lhsT=triu[:, :], rhs=sp_sb[:, :], start=True, stop=True)
                    P.add('pe', fn, reads=[spk, 'tris', 'triu'], writes=[pbk])
                    pb3 = pb[:, 0:256].rearrange("p (a t) -> p a t", a=2)
                    P.stage('m231')
                    P.add('act', lambda e, pb3=pb3, cc=cc: e.activation(out=bT_sb[:, :, cc], in_=pb3, func=AF.Identity), reads=[pbk], writes=[('bT', c)])
                    P.stage('m233')
                    P.add('act', lambda e, pb3=pb3, c=c: e.activation(out=small[:, 2 * c:2 * c + 2].rearrange("p (a o) -> p a o", o=1), in_=pb3[:, :, 127:128], func=AF.Exp),
                          reads=[pbk], writes=[('ebl', c)])
                    P.stage('m234')
                    P.add('act', lambda e, pb=pb, er_sb=er_sb: e.activation(out=er_sb, in_=pb[:, 256:512], func=AF.Exp), reads=[pbk], writes=[erk])
                    P.stage('m24')
                    pk, pkk = ps_next()

                    def fn(e, pk=pk, cc=cc, s_qk=s_qk):
                        ins = None
                        for k in range(KD):
                            ins = e.matmul(pk[:, 0:256], lhsT=hTt[:, k, cc], rhs=wslot(s_qk)[:, k, 256:512], start=(k == 0), stop=(k == KD - 1))
                        return ins
                    P.add('pe', fn, reads=wsk(s_qk, [2, 3]) + hkeys, writes=[pkk])
                    P.add('dve', lambda e, pk=pk, c=c, er_sb=er_sb: e.tensor_tensor(out=khat[:, c, :], in0=pk[:, 0:256], in1=er_sb, op=ALU.mult),
                          reads=[pkk, erk], writes=[('khat', c)])
                P.stage('m3')
                bkeys = [('bT', c) for c in range(4)]
                for p in range(2):
                    pq, pqk = ps_next()

                    def fn(e, pq=pq, p=p, s_qk=s_qk):
                        ins = None
                        for k in range(KD):
                            ins = e.matmul(pq[:, :], lhsT=wslot(s_qk)[:, k, p * 128:(p + 1) * 128], rhs=hTt[:, k, :], start=(k == 0), stop=(k == KD - 1))
                        return ins
                    P.add('pe', fn, reads=wsk(s_qk, [p]) + hkeys, writes=[pqk])
                    pk2, pk2k = ps_next()

                    def fn(e, pk2=pk2, p=p, s_qk=s_qk):
                        ins = None
                        for k in range(KD):
                            ins = e.matmul(pk2[:, :], lhsT=wslot(s_qk)[:, k, 256 + p * 128:256 + (p + 1) * 128], rhs=hTt[:, k, :], start=(k == 0), stop=(k == KD - 1))
                        return ins
                    P.add('pe', fn, reads=wsk(s_qk, [2 + p]) + hkeys, writes=[pk2k])
                    P.add('act', lambda e, p=p: e.activation(out=E1, in_=bT_sb[:, p, :], func=AF.Exp), reads=bkeys, writes=[E1K])
                    P.add('dve', lambda e, p=p, pq=pq: e.scalar_tensor_tensor(out=qtT[:, p, :], in0=pq[:, :], scalar=0.125, in1=E1, op0=ALU.mult, op1=ALU.mult),
                          reads=[pqk, E1K], writes=[('qt', p)])
                    P.add('act', lambda e, p=p: e.activation(out=E1, in_=bT_sb[:, p, :], func=AF.Exp, scale=-1.0), reads=bkeys, writes=[E1K])
                    P.add('dve', lambda e, p=p, pk2=pk2: e.tensor_tensor(out=ktT[:, p, :], in0=pk2[:, :], in1=E1, op=ALU.mult),
                          reads=[pk2k, E1K], writes=[('kt', p)])
                if t > 0:
                    P.stage('m5')
                    gate_norm(t - 1)
                if t + 1 < NT:
                    P.stage('m1')
                    norm_for(t + 1)
                elif last_hook is not None:
                    last_hook()
                for c in range(4):
                    gc = t * 4 + c
                    cc = slice(c * 128, (c + 1) * 128)
                    gcols = slice(t * 512 + c * 128, t * 512 + (c + 1) * 128)
                    P.stage('m4s')
                    for hh in range(2):
                        pscb, psck = ps_next()

                        def fn(e, hh=hh, pscb=pscb, cc=cc):
                            ins = None
                            for p in range(2):
                                ins = e.matmul(pscb[:, p * 128:(p + 1) * 128], lhsT=ktT[hh * 64:(hh + 1) * 64, p, cc], rhs=qtT[hh * 64:(hh + 1) * 64, p, cc],
                                               start=True, stop=True)
                            return ins
                        P.add('pe', fn, reads=[('kt', 0), ('kt', 1), ('qt', 0), ('qt', 1)], writes=[psck])
                        P.add('dve', lambda e, hh=hh, pscb=pscb, c=c: e.tensor_tensor(out=scm[:, (c % 2) * 2 + hh, :], in0=pscb[:, 0:256], in1=mask[:, :], op=ALU.mult),
                              reads=[psck, 'mask'], writes=[('scm', c % 2, hh)])
                    P.stage('m4c')
                    j = c
                    s = ws_alloc()
                    load_cols(s, 0, wmi_d, j * 128, 128, off=0)
                    load_cols(s, 0, wmi_d, 512 + j * 128, 128, off=128)
                    load_cols(s, 1, wmi_d, 1024 + j * 128, 128, off=0)
                    load_cols(s, 1, wmi_d, 2560 + j * 128, 128, off=128)
                    pp = [ps_next() for _ in range(4)]
                    for i4 in range(4):
                        pbank, pkey = pp[i4]

                        def fn(e, pbank=pbank, i4=i4, s=s):
                            ins = None
                            for k in range(KD):
                                ins = e.matmul(pbank[:, :], lhsT=wslot(s)[:, k, i4 * 128:(i4 + 1) * 128], rhs=hTt[:, k, :], start=(k == 0), stop=(k == KD - 1))
                            return ins
                        P.add('pe', fn, reads=wsk(s, [i4]) + hkeys, writes=[pkey])
                    (pcb, pcbk), (pcc, pcck), (pcv, pcvk), (pg_, pgk2) = pp
                    P.add('act', lambda e, pcc=pcc: e.activation(out=E1, in_=pcc[:, :], func=AF.Identity), reads=[pcck], writes=[E1K])
                    P.add('act', lambda e, j=j: e.activation(out=ubuf[:, 0:2], in_=prm[:, P_UH + 2 * j:P_UH + 2 * j + 2], func=AF.Identity), reads=['uhalo'], writes=['ubuf'])
                    P.add('dve', lambda e, pcv=pcv: e.tensor_tensor(out=ubuf[:, 2:514], in0=E1, in1=pcv[:, :], op=ALU.mult), reads=[E1K, pcvk, 'ubuf'], writes=['ubuf'])
                    P.add('act', lambda e, j=j: e.activation(out=E1, in_=ubuf[:, 2:514], func=AF.Identity, scale=cw(2, j)), reads=['ubuf', 'vecs'], writes=[E1K])
                    P.add('dve', lambda e, j=j: e.scalar_tensor_tensor(out=E2, in0=ubuf[:, 1:513], scalar=cw(1, j), in1=E1, op0=ALU.mult, op1=ALU.add),
                          reads=['ubuf', E1K, 'vecs'], writes=[E2K])
                    P.add('dve', lambda e, j=j: e.scalar_tensor_tensor(out=E1, in0=ubuf[:, 0:512], scalar=cw(0, j), in1=E2, op0=ALU.mult, op1=ALU.add),
                          reads=['ubuf', E2K, 'vecs'], writes=[E1K])
                    P.add('dve', lambda e, j=j, pcb=pcb, cols=cols: e.tensor_tensor(out=yT[:, j, cols], in0=pcb[:, :], in1=E1, op=ALU.mult), reads=[pcbk, E1K], writes=[('yT', j, t)])
                    P.add('act', lambda e, j=j: e.activation(out=prm[:, P_UH + 2 * j:P_UH + 2 * j + 2], in_=ubuf[:, 512:514], func=AF.Identity), reads=['ubuf'], writes=['uhalo'])
                    P.add('act', lambda e, j=j, pg_=pg_, cols=cols: e.activation(out=yT[:, 4 + j, cols], in_=pg_[:, :], func=AF.Silu), reads=[pgk2], writes=[('yT', 4 + j, t)])
                    P.stage('m4o')
                    sbuf_i = gc % 2
                    for hh in range(2):
                        po, pok = ps_next()

                        def fn(e, hh=hh, po=po, c=c, cc=cc, sbuf_i=sbuf_i):
                            ins = None
                            for p in range(2):
                                h = 2 * p + hh
                                e.matmul(po[:, p * 128:(p + 1) * 128], lhsT=vtok[:, c, h * 128:(h + 1) * 128], rhs=scm[:, (c % 2) * 2 + hh, p * 128:(p + 1) * 128],
                                         start=True, stop=False)
                                ins = e.matmul(po[:, p * 128:(p + 1) * 128], lhsT=S_bf[hh * 64:(hh + 1) * 64, sbuf_i * 2 + p, hh * 128:(hh + 1) * 128],
                                               rhs=qtT[hh * 64:(hh + 1) * 64, p, cc], start=False, stop=True)
                            return ins
                        P.add('pe', fn, reads=[('vtok', c), ('scm', c % 2, hh), ('Sbf', sbuf_i, 0), ('Sbf', sbuf_i, 1), ('qt', 0), ('qt', 1)], writes=[pok])
                        P.add('act', lambda e, hh=hh, po=po, cc=cc: e.activation(out=oT[:, hh:4:2, cc], in_=po[:, 0:256].rearrange("p (a t) -> p a t", a=2), func=AF.Identity),
                              reads=[pok], writes=[('oT', hh, t % 2, c)])
                    P.stage('m4u')
                    pu_, puk_ = ps_next()

                    def fn(e, pu_=pu_, c=c):
                        e.matmul(pu_[:, 0:256], lhsT=khat[:, c, 0:128], rhs=vtok[:, c, 0:256], start=True, stop=True)
                        return e.matmul(pu_[:, 256:512], lhsT=khat[:, c, 128:256], rhs=vtok[:, c, 256:512], start=True, stop=True)
                    P.add('pe', fn, reads=[('khat', c), ('vtok', c)], writes=[puk_])
                    for p in range(2):
                        P.add('dve', lambda e, p=p, pu_=pu_, c=c: e.scalar_tensor_tensor(out=S_sb[:, p, :], in0=S_sb[:, p, :], scalar=small[:, 2 * c + p:2 * c + p + 1],
                                                                                         in1=pu_[:, p * 256:(p + 1) * 256], op0=ALU.mult, op1=ALU.add),
                              reads=[('S', p), puk_, ('ebl', c)], writes=[('S', p)])
                        P.add('act', lambda e, p=p, sbuf_i=sbuf_i: e.activation(out=S_bf[:, (1 - sbuf_i) * 2 + p, :], in_=S_sb[:, p, :], func=AF.Identity),
                              reads=[('S', p)], writes=[('Sbf', 1 - sbuf_i, p)])
            if not skip_norm0:
                P.stage('m1')
                norm_for(0)
            for t in range(NT):
                tile_body(t)
            P.stage('m5')
            gate_norm(NT - 1)
            P.off = set()
            P.stage('mix_tiles')
            if prefix:
                ws_state['nslots'] = 2

        def prefix_tail():
            hTt = [D_hTt0, hTt1][(NT - 1) % 2]
            hkeys = [('hTt', (NT - 1) % 2, k) for k in range(KD)]
            for j in range(4):
                s = ws_alloc()
                load_cols(s, 0, wmi_d, 512 + j * 128, 128, off=0)
                load_cols(s, 0, wmi_d, 1024 + j * 128, 128, off=128)
                pc_, pck_ = ps_next()

                def fn(e, s=s, pc_=pc_):
                    ins = None
                    for i2 in range(2):
                        for k in range(KD):
                            ins = e.matmul(pc_[:, 2 * i2:2 * i2 + 2], lhsT=wslot(s)[:, k, i2 * 128:(i2 + 1) * 128], rhs=hTt[:, k, 510:512],
                                           start=(k == 0), stop=(k == KD - 1))
                    return ins
                P.add('pe', fn, reads=wsk(s, [0, 1]) + hkeys, writes=[pck_])
                P.add('act', lambda e, pc_=pc_, j=j: e.activation(out=small[:, 48 + 2 * j:50 + 2 * j], in_=pc_[:, 0:2], func=AF.Identity), reads=[pck_], writes=[('ut', j)])
                P.add('dve', lambda e, pc_=pc_, j=j: e.tensor_tensor(out=spst[:, 512 + 2 * j:514 + 2 * j], in0=small[:, 48 + 2 * j:50 + 2 * j], in1=pc_[:, 2:4], op=ALU.mult),
                      reads=[pck_, ('ut', j)], writes=['spstB'])
            P.add('act', lambda e: e.activation(out=spst[:, 0:512].rearrange("p (a t) -> p a t", a=2), in_=S_sb[:, :, :], func=AF.Identity),
                  reads=[('S', 0), ('S', 1)], writes=['spstA'])

        P.stage('ada0')
        for _ in range(4):
            ada_block()
        make_A(P_A1, 1, V_N1)
        P.stage('pre_ffn1')
        first_norm = lambda t: norm_for(0) if t == 0 else None
        ffn(w1i_d, w1o_d, P_A1, 0, P_GH1, [ada2] * 12, lambda: make_half(P_GH1, 2), post_tile=lambda t: (make_A(P_A2, 4, V_NM), norm_for(0)) if t == 0 else None)

        def load_x_and_norm1():
            P.stage('ffn1')
            for k in range(KD):
                P.add('sp', lambda e, k=k: e.dma_start(out=xT[:, k, :], in_=xT_d[k * 128:(k + 1) * 128, :]),
                      writes=[('xT', k, t) for t in range(NT)], dma_key=f'x{k}')
            for t in range(NT):
                ffn_norm(t, P_A1, 0)
        mixer_tiles(True, skip_norm0=True, last_hook=load_x_and_norm1)
        P.stage('pre_tail')
        prefix_tail()
        P.stage('ffn1')
        P.fence()
        ffn(w1i_d, w1o_d, P_A1, 0, P_GH1, [ada2] * 12, None, do_norm=False, post_tile=first_norm)
        mixer_tiles(False, skip_norm0=True)

        P.stage('mixout')
        make_A(P_A3, 7, V_N2)
        wm_slots = [ws_alloc(), ws_alloc()]
        for i, s in enumerate(wm_slots):
            dst = wslot(s)[:, :, :].rearrange("p k c -> p (k c)").rearrange("p (k d) -> p k d", k=4)
            srcv = wmo_d[i * 512:(i + 1) * 512, :].rearrange("(k p) d -> p k d", p=128)
            P.add('pool', lambda e, dst=dst, srcv=srcv: e.dma_start(out=dst, in_=srcv), writes=wsk(s, range(4)), dma_key=f'wm{s}')
        wm_view = [wslot(s)[:, :, :].rearrange("p k c -> p (k c)").rearrange("p (k d) -> p k d", k=4) for s in wm_slots]
        for t in range(NT):
            cols = slice(t * 512, (t + 1) * 512)
            for d in range(KD):
                po, pok = ps_next()

                def fn(e, d=d, cols=cols, po=po):
                    ins = None
                    for kc in range(8):
                        ins = e.matmul(po[:, :], lhsT=wm_view[kc // 4][:, kc % 4, d * 128:(d + 1) * 128], rhs=yT[:, kc, cols], start=(kc == 0), stop=(kc == 7))
                    return ins
                P.add('pe', fn, reads=[('ws', s, q) for s in wm_slots for q in range(4)] + [('yT', kc, t) for kc in range(8)], writes=[pok])
                P.add('dve', lambda e, d=d, cols=cols, po=po: e.scalar_tensor_tensor(out=xT[:, d, cols], in0=po[:, :], scalar=ada[:, 40 + d:41 + d],
                                                                                    in1=xT[:, d, cols], op0=ALU.mult, op1=ALU.add),
                      reads=[pok, ('xT', d, t)] + ada_keys(5), writes=[('xT', d, t)])
            ffn_norm(t, P_A3, 6)

        def final_tile(t):
            cols = slice(t * 512, (t + 1) * 512)
            ps, psk = ps_next()
            for k in range(KD):
                b = k % 2
                P.add('act', lambda e, k=k, b=b, cols=cols: e.activation(out=sqt[:, b, :], in_=xT[:, k, cols], func=AF.Square),
                      reads=[('xT', k, t)], writes=[('sqt', b)])
                P.add('pe', lambda e, k=k, b=b, ps=ps: e.matmul(ps[:, :], lhsT=ones_bf[:, :], rhs=sqt[:, b, :], start=(k == 0), stop=(k == KD - 1)),
                      reads=[('sqt', b), 'ones'], writes=[psk])
            rs, rsk = tmp_next()
            P.add('act', lambda e, rs=rs, ps=ps: e.activation(out=rs, in_=ps[:, :], func=AF.Ln, scale=1.0 / D, bias=prm[:, P_EPS:P_EPS + 1]),
                  reads=[psk, 'eps'], writes=[rsk])
            P.add('act', lambda e, rs=rs: e.activation(out=rs, in_=rs, func=AF.Exp, scale=-0.5), reads=[rsk], writes=[rsk])
            for k in range(KD):
                P.add('dve', lambda e, k=k, rs=rs, cols=cols: e.scalar_tensor_tensor(out=xT[:, k, cols], in0=xT[:, k, cols], scalar=vecs[:, V_NF + k:V_NF + k + 1],
                                                                                    in1=rs, op0=ALU.mult, op1=ALU.mult),
                      reads=[('xT', k, t), rsk, 'vecs'], writes=[('xT', k, t)])
                P.add('sp', lambda e, k=k, cols=cols: e.dma_start(out=out_d[k * 128:(k + 1) * 128, cols], in_=xT[:, k, cols]),
                      reads=[('xT', k, t)], writes=[('out', k, t)], dma_key=f'o{k}')

        P.stage('ffn2')
        P.fence()
        ws_state['nslots'] = 2
        ffn(w2i_d, w2o_d, P_A3, 6, P_GH3, [ada2] * 4, lambda: make_half(P_GH3, 8), do_norm=False, post_tile=final_tile)
        P.emit(nc, st)
    return nc


_CACHE = {}


def kernel(x, c, w_ada, b_ada, norm_ffn1, w_ffn1_in, w_ffn1_out, norm_mix, w_mix_in, conv_w, w_gk2, b_gk,
           gla_norm, w_mix_out, norm_ffn2, w_ffn2_in, w_ffn2_out, norm_final):
    f = lambda a: np.ascontiguousarray(np.asarray(a, dtype=np.float32))
    x = f(x)
    B, T, _ = x.shape
    TT = T // 2
    if T not in _CACHE:
        _CACHE[T] = build(T)
    nc = _CACHE[T]

    def fm(v):
        v = f(v).reshape(-1, 128)
        return v.T

    jj, ii = np.meshgrid(np.arange(128), np.arange(128), indexing='ij')
    tris = np.where(jj <= ii, -1.0 / 16.0, 0.0).astype(np.float32)
    triu = np.where(jj > ii, -1.0 / 16.0, 0.0).astype(np.float32)
    mask = np.tile(np.where(jj <= ii, 1.0, 0.0).astype(np.float32), (1, 2))
    wgk2a = np.concatenate([f(w_gk2)[0], f(b_gk)[0][None, :]], axis=0)
    shared = {
        "w_ada": f(w_ada)[0], "w_ffn1_in": f(w_ffn1_in)[0], "w_ffn1_out": f(w_ffn1_out)[0],
        "w_mix_in": f(w_mix_in)[0], "wgk2a": f(wgk2a), "w_mix_out": f(w_mix_out)[0],
        "w_ffn2_in": f(w_ffn2_in)[0], "w_ffn2_out": f(w_ffn2_out)[0],
        "tris": tris, "triu": triu, "mask": mask,
    }
    cwv = f(conv_w)[0]
    in_maps = []
    for core in range(NCORES):
        b, half = core // 2, core % 2
        vecs = np.zeros((128, NV), np.float32)
        vecs[:, 0:8] = fm(f(c)[b])
        vecs[:, 8:80] = fm(f(b_ada)[0])
        vecs[:, 80:88] = fm(f(norm_ffn1)[0])
        vecs[:, 88:96] = fm(f(norm_mix)[0])
        vecs[:, 96:104] = fm(f(norm_ffn2)[0])
        vecs[:, 104:112] = fm(f(norm_final))
        for kk in range(3):
            vecs[:, 112 + kk * 4:112 + kk * 4 + 4] = fm(cwv[kk])
        vecs[:, 124] = f(gla_norm)[0]
        vecs[:, 125] = float(half)
        m = dict(shared)
        m["xT"] = np.ascontiguousarray(x[b, half * TT:(half + 1) * TT, :].T)
        m["xTp"] = np.ascontiguousarray(x[b, 0:TT, :].T) if half == 1 else np.zeros((D, TT), np.float32)
        m["vecs"] = vecs
        in_maps.append(m)
    res = run_bass_kernel_spmd(nc, in_maps, core_ids=list(range(NCORES)))
    out = np.empty((B, T, D), np.float32)
    for core in range(NCORES):
        b, half = core // 2, core % 2
        out[b, half * TT:(half + 1) * TT, :] = res.results[core]["outT"].T
    return out
```

```python
from contextlib import ExitStack
import numpy as np
import concourse.bass as bass
import concourse.mybir as mybir
from concourse.bass_utils import run_bass_kernel_spmd

F32 = mybir.dt.float32
BF16 = mybir.dt.bfloat16
AF = mybir.ActivationFunctionType
ALU = mybir.AluOpType

D = 1024
KD = 8
DFF = 2816
NF = 22
BATCH = 4
SEQ = 4096
NCORES = 8
EPS = 1e-6
NV = 129
XW = 520

ENGS = ['pe', 'act', 'dve', 'pool', 'sp']
EPOCH = 30000

STAGES = None


class Op:
    __slots__ = ('eng', 'fn', 'deps', 'dma_key', 'signals', 'val')


class Prog:
    def __init__(self):
        self.streams = {e: [] for e in ENGS}
        self.last_w = {}
        self.readers = {}
        self.dma_count = {}
        self.nops = 0
        self.pending_fence = {}

    def fence(self):
        if not self.enabled:
            return
        last = [self.streams[e][-1] for e in ('pe', 'act', 'dve') if self.streams[e]]
        for e in ENGS:
            self.pending_fence[e] = list(last)

    enabled = True

    off = set()

    def stage(self, name):
        if name in self.off:
            self.enabled = False
            return
        self.enabled = STAGES is None or name in STAGES or (name[0] == 'm' and name[1:].isdigit() and 'mix_tiles' in STAGES and not any(
            x[0] == 'm' and x[1:].isdigit() for x in STAGES))

    def add(self, eng, fn, reads=(), writes=(), dma_key=None, nofence=False):
        if not self.enabled:
            return None
        op = Op()
        op.eng = eng
        op.fn = fn
        op.dma_key = dma_key
        op.signals = False
        op.val = 0
        psr = [r for r in reads if isinstance(r, tuple) and r[0] == 'ps']
        if psr:
            writes = list(writes) + psr
        deps = set()
        for r in reads:
            w = self.last_w.get(r)
            if w is not None:
                deps.add(w)
        for r in writes:
            w = self.last_w.get(r)
            if w is not None:
                deps.add(w)
            rd = self.readers.get(r)
            if rd:
                deps.update(rd.values())
        if eng == 'pe' and dma_key is None:
            deps = {d for d in deps if not (d.eng == 'pe' and d.dma_key is None)}
        if self.pending_fence.get(eng) and not nofence:
            for d in self.pending_fence[eng]:
                if not (eng == 'pe' and d.eng == 'pe'):
                    deps.add(d)
            self.pending_fence[eng] = None
        op.deps = deps
        for d in deps:
            d.signals = True
        for r in reads:
            rk = eng if dma_key is None else ('dma', self.nops)
            self.readers.setdefault(r, {})[rk] = op
        for r in writes:
            self.last_w[r] = op
            self.readers[r] = {}
        if dma_key is not None:
            self.dma_count[dma_key] = self.dma_count.get(dma_key, 0) + 16
            op.val = self.dma_count[dma_key]
        self.streams[eng].append(op)
        self.nops += 1
        return op

    def emit(self, nc, st):
        nep = {}
        for e in ENGS:
            cnt = 0
            for op in self.streams[e]:
                if op.dma_key is None and op.signals:
                    cnt += 1
                    op.val = cnt
            nep[e] = cnt // EPOCH + 1
        sems = {e: [st.enter_context(nc.semaphore(f"s_{e}{i}")) for i in range(nep[e])] for e in ENGS}
        dsems = {k: st.enter_context(nc.semaphore(f"d_{k}")) for k in self.dma_count}
        block = st.enter_context(nc.Block())

        def sem_of(d):
            if d.dma_key is not None:
                return ('d', d.dma_key), dsems[d.dma_key], d.val
            ep = (d.val - 1) // EPOCH
            return (d.eng, ep), sems[d.eng][ep], d.val - ep * EPOCH

        streams = self.streams
        dma_count = self.dma_count

        def mk(e):
            def body(engobj):
                waited = {}
                for op in streams[e]:
                    need = {}
                    for d in op.deps:
                        k, s, v = sem_of(d)
                        if k not in need or need[k][1] < v:
                            need[k] = (s, v)
                    for k, (s, v) in need.items():
                        if waited.get(k, 0) >= v:
                            continue
                        engobj.wait_ge(s, v)
                        waited[k] = v
                    ins = op.fn(engobj)
                    if op.dma_key is not None:
                        ins.then_inc(dsems[op.dma_key], 16)
                    elif op.signals:
                        k, s, v = sem_of(op)
                        ins.then_inc(s, 1)
                if e == 'sp':
                    for k, c in dma_count.items():
                        engobj.wait_ge(dsems[k], c)
            return body

        block.tensor(mk('pe'))
        block.scalar(mk('act'))
        block.vector(mk('dve'))
        block.gpsimd(mk('pool'))
        block.sync(mk('sp'))


def build(seq):
    TT = seq // 2
    NT = TT // 512
    nc = bass.Bass("TRN2", target_bir_lowering=False)

    def din(name, shape):
        return nc.dram_tensor(name, shape, F32, kind="ExternalInput").ap()

    xT_d = din("xT", [D, TT])
    xTp_d = din("xTp", [D, TT])
    vecs_d = din("vecs", [128, NV])
    wada_d = din("w_ada", [D, 9 * D])
    w1i_d = din("w_ffn1_in", [D, 2 * DFF])
    w1o_d = din("w_ffn1_out", [DFF, D])
    wmi_d = din("w_mix_in", [D, 3088])
    wgk2_d = din("wgk2a", [17, 256])
    wmo_d = din("w_mix_out", [D, D])
    w2i_d = din("w_ffn2_in", [D, 2 * DFF])
    w2o_d = din("w_ffn2_out", [DFF, D])
    tris_d = din("tris", [128, 128])
    triu_d = din("triu", [128, 128])
    mask_d = din("mask", [128, 256])
    out_d = nc.dram_tensor("outT", [D, TT], F32, kind="ExternalOutput").ap()

    P = Prog()
    with ExitStack() as st:
        def sb(name, shape, dt):
            return st.enter_context(nc.sbuf_tensor(name, shape, dt))

        xT = sb("xT_sb", [128, KD, TT], F32)
        A_raw = sb("arenaA", [128, 4 * TT], F32)
        B_raw = sb("arenaB", [128, max(4 * TT, 8192)], F32)
        D_raw = sb("arenaD", [128, 9760], F32)
        WA = sb("wada_slots", [128, 2, KD, 128], BF16)
        WS = sb("wslots", [128, 2, KD, 512], BF16)
        TMP = sb("tmp", [128, 6, 512], F32)
        vecs = sb("vecs_sb", [128, NV], F32)
        ada = sb("ada_sb", [128, 72], F32)
        prm = sb("prm_sb", [128, 64], F32)
        cact = sb("cact", [128, KD], BF16)
        ones_bf = sb("ones_bf", [128, 128], BF16)
        tris = sb("tris_sb", [128, 128], F32)
        triu = sb("triu_sb", [128, 128], F32)
        mask = sb("mask_sb", [128, 256], F32)
        wgk2 = sb("wgk2_sb", [32, 256], F32)
        wgk = sb("wgk_sb", [128, KD, 16], BF16)
        sqt = sb("sqt", [128, 2, 512], BF16)
        small = sb("small", [128, 64], F32)
        spst = sb("spst", [128, XW], F32)

        PS = [st.enter_context(nc.psum_tensor(f"ps{i}", [128, 512], F32)) for i in range(8)]
        ps_state = {'i': 0}

        def ps_next():
            i = ps_state['i'] % 8
            ps_state['i'] += 1
            return PS[i], ('ps', i)

        tmp_state = {'i': 0}

        def tmp_next():
            i = tmp_state['i'] % 4
            tmp_state['i'] += 1
            return TMP[:, i, :], ('tmp', i)

        A_bf = A_raw[:, :].bitcast(BF16).rearrange("p (k t) -> p k t", k=KD)
        B_bf = B_raw[:, 0:4 * TT].bitcast(BF16).rearrange("p (k t) -> p k t", k=KD)
        oTb = [B_raw[:, b * 2048:(b + 1) * 2048].rearrange("p (h t) -> p h t", h=4) for b in range(2)]
        hTt1 = B_raw[:, 4096:6144].bitcast(BF16).rearrange("p (k t) -> p k t", k=KD)
        WS3 = B_raw[:, 6144:8192].bitcast(BF16).rearrange("p (k c) -> p k c", k=KD)

        def wslot(s):
            return WS3 if s == 2 else WS[:, s]
        D_bf = D_raw[:, :].bitcast(BF16)
        wo_buf = [D_bf[:, i * 6144:(i + 1) * 6144].rearrange("p (f d) -> p f d", f=6) for i in range(2)]
        o0 = 0
        hTt = D_bf[:, o0:o0 + 4096].rearrange("p (k t) -> p k t", k=KD)
        o0 += 4096
        qtT = D_bf[:, o0:o0 + 1024].rearrange("p (a t) -> p a t", a=2)
        o0 += 1024
        ktT = D_bf[:, o0:o0 + 1024].rearrange("p (a t) -> p a t", a=2)
        o0 += 1024
        khat = D_bf[:, o0:o0 + 1024].rearrange("p (a t) -> p a t", a=4)
        o0 += 1024
        vtok = D_bf[:, o0:o0 + 2048].rearrange("p (a t) -> p a t", a=4)
        o0 += 2048
        scm = D_bf[:, o0:o0 + 1024].rearrange("p (a t) -> p a t", a=4)
        o0 += 1024
        S_bf = D_bf[:, o0:o0 + 1024].rearrange("p (a t) -> p a t", a=4)
        o0 += 1024
        f0 = o0 // 2
        bT_sb = D_raw[:, f0:f0 + 1024].rearrange("p (a t) -> p a t", a=2)
        f0 += 1024
        NB = 3
        sp_bufs = [D_raw[:, f0 + 256 * i:f0 + 256 * (i + 1)] for i in range(NB)]
        f0 += 256 * NB
        er_bufs = [D_raw[:, f0 + 256 * i:f0 + 256 * (i + 1)] for i in range(NB)]
        f0 += 256 * NB
        ubuf = D_raw[:, f0:f0 + 516]
        f0 += 516
        S_sb = D_raw[:, f0:f0 + 512].rearrange("p (a t) -> p a t", a=2)
        f0 += 512
        gk_aug = D_raw[0:32, f0:f0 + 512]
        f0 += 512
        assert f0 <= 9760, f0
        V_C, V_BADA, V_N1, V_NM, V_N2, V_NF, V_CW, V_GN, V_FLAG = 0, 8, 80, 88, 96, 104, 112, 124, 125
        P_A1, P_A2, P_A3, P_GH1, P_GH3, P_EPS = 0, 8, 16, 24, 32, 40
        P_CB01 = 44
        P_UH = 52

        P.add('sp', lambda e: e.dma_start(out=vecs[:, :], in_=vecs_d[:, :]), writes=['vecs'], dma_key='vecs')
        for k in range(KD):
            P.add('sp', lambda e, k=k: e.dma_start(out=xT[:, k, :], in_=xTp_d[k * 128:(k + 1) * 128, :]),
                  writes=[('xT', k, t) for t in range(NT)], dma_key=f'x{k}')
        P.add('sp', lambda e: e.dma_start(out=tris[:, :], in_=tris_d[:, :]), writes=['tris'], dma_key='c0')
        P.add('sp', lambda e: e.dma_start(out=triu[:, :], in_=triu_d[:, :]), writes=['triu'], dma_key='c1')
        P.add('sp', lambda e: e.dma_start(out=mask[:, :], in_=mask_d[:, :]), writes=['mask'], dma_key='c2')
        P.add('sp', lambda e: e.dma_start(out=wgk2[0:17, :], in_=wgk2_d[:, :]), writes=['wgk2'], dma_key='c3')
        P.add('dve', lambda e: e.memset(ones_bf[:, :], 1.0), writes=['ones'])
        P.add('dve', lambda e: e.memset(prm[:, P_EPS:P_EPS + 1], EPS), writes=['eps'])
        P.add('act', lambda e: e.activation(out=cact[:, :], in_=vecs[:, V_C:V_C + 8], func=AF.Silu),
              reads=['vecs'], writes=['cact'])

        ws_state = {'n': 0, 'nslots': 2}

        def ws_alloc():
            s = ws_state['n'] % ws_state['nslots']
            ws_state['n'] += 1
            return s

        def load_cols(slot, half, src, c0, ncols, off=0):
            co = half * 256 + off
            dst = wslot(slot)[:, :, co: co + ncols]
            srcv = src.rearrange("(k p) c -> p k c", p=128)[:, :, c0:c0 + ncols]
            qs = list(range(co // 128, (co + ncols) // 128))
            P.add('pool', lambda e: e.dma_start(out=dst, in_=srcv),
                  writes=[('ws', slot, q) for q in qs], dma_key=f'ws{slot}_{qs[0]}', nofence=(slot != 2))

        def wsk(slot, qs):
            return [('ws', slot, q) for q in qs]

        ada_state = {'blk': 0, 'c128': 16}
        wa_state = {'n': 0}

        def ada_block():
            blk = ada_state['blk']
            ada_state['blk'] += 1
            s = ws_alloc()
            load_cols(s, 0, wada_d, blk * 512, 256)
            load_cols(s, 1, wada_d, blk * 512 + 256, 256)
            pa, pak = ps_next()

            def fn(e, s=s, pa=pa):
                ins = None
                for cc in range(4):
                    for k in range(KD):
                        ins = e.matmul(pa[:, cc:cc + 1], lhsT=wslot(s)[:, k, cc * 128:(cc + 1) * 128], rhs=cact[:, k:k + 1],
                                       start=(k == 0), stop=(k == KD - 1))
                return ins
            P.add('pe', fn, reads=wsk(s, range(4)) + ['cact'], writes=[pak])
            P.add('dve', lambda e, blk=blk, pa=pa: e.tensor_tensor(out=ada[:, blk * 4:blk * 4 + 4], in0=pa[:, 0:4],
                                                                   in1=vecs[:, V_BADA + blk * 4:V_BADA + blk * 4 + 4], op=ALU.add),
                  reads=[pak, 'vecs'], writes=[('ada', blk * 4 + i) for i in range(4)])

        def ada_chunk():
            c = ada_state['c128']
            ada_state['c128'] += 1
            s = wa_state['n'] % 2
            wa_state['n'] += 1
            srcv = wada_d.rearrange("(k p) c -> p k c", p=128)[:, :, c * 128:(c + 1) * 128]
            P.add('pool', lambda e: e.dma_start(out=WA[:, s, :, :], in_=srcv), writes=[('wa', s)], dma_key=f'wa{s}')
            pa, pak = ps_next()

            def fn(e):
                ins = None
                for k in range(KD):
                    ins = e.matmul(pa[:, 0:1], lhsT=WA[:, s, k, :], rhs=cact[:, k:k + 1], start=(k == 0), stop=(k == KD - 1))
                return ins
            P.add('pe', fn, reads=[('wa', s), 'cact'], writes=[pak])
            P.add('dve', lambda e: e.tensor_tensor(out=ada[:, c:c + 1], in0=pa[:, 0:1], in1=vecs[:, V_BADA + c:V_BADA + c + 1], op=ALU.add),
                  reads=[pak, 'vecs'], writes=[('ada', c)])

        def ada2():
            ada_chunk()
            ada_chunk()

        def ada_keys(n):
            return [('ada', 8 * n + i) for i in range(8)]

        def make_A(col, n_sc, vcol):
            P.add('dve', lambda e: e.scalar_tensor_tensor(out=prm[:, col:col + 8], in0=ada[:, n_sc * 8:n_sc * 8 + 8], scalar=1.0,
                                                          in1=vecs[:, vcol:vcol + 8], op0=ALU.add, op1=ALU.mult),
                  reads=ada_keys(n_sc) + ['vecs'], writes=[('prm', col)])

        def make_half(col, n_g):
            P.add('dve', lambda e: e.tensor_scalar(out=prm[:, col:col + 8], in0=ada[:, n_g * 8:n_g * 8 + 8], scalar1=0.5, scalar2=None,
                                                   op0=ALU.mult),
                  reads=ada_keys(n_g), writes=[('prm', col)])

        def norm_tile(t, dst, dst_key, acol, sh_n, sh_reads):
            cols = slice(t * 512, (t + 1) * 512)
            ps, psk = ps_next()
            for k in range(KD):
                b = k % 2
                P.add('act', lambda e, k=k, b=b: e.activation(out=sqt[:, b, :], in_=xT[:, k, cols], func=AF.Square),
                      reads=[('xT', k, t)], writes=[('sqt', b)])
                P.add('pe', lambda e, k=k, b=b: e.matmul(ps[:, :], lhsT=ones_bf[:, :], rhs=sqt[:, b, :],
                                                         start=(k == 0), stop=(k == KD - 1)),
                      reads=[('sqt', b), 'ones'], writes=[psk])
            rs, rsk = TMP[:, 4, :], ('tmp', 4)
            P.add('act', lambda e: e.activation(out=rs, in_=ps[:, :], func=AF.Ln, scale=1.0 / D, bias=prm[:, P_EPS:P_EPS + 1]),
                  reads=[psk, 'eps'], writes=[rsk])
            P.add('act', lambda e: e.activation(out=rs, in_=rs, func=AF.Exp, scale=-0.5), reads=[rsk], writes=[rsk])
            for k in range(KD):
                tm, tmk = tmp_next()
                P.add('dve', lambda e, k=k, tm=tm: e.scalar_tensor_tensor(out=tm, in0=xT[:, k, cols], scalar=prm[:, acol + k:acol + k + 1],
                                                                          in1=rs, op0=ALU.mult, op1=ALU.mult),
                      reads=[('xT', k, t), ('prm', acol), rsk], writes=[tmk])
                P.add('act', lambda e, k=k, tm=tm: e.activation(out=dst(k), in_=tm, func=AF.Identity,
                                                                bias=ada[:, sh_n * 8 + k:sh_n * 8 + k + 1], scale=1.0),
                      reads=[tmk] + sh_reads, writes=[dst_key(k)])

        def ffn_norm(t, acol, sh_n):
            norm_tile(t, lambda k, t=t: A_bf[:, k, t * 512:(t + 1) * 512], lambda k, t=t: ('hT', k, t),
                      acol, sh_n, ada_keys(sh_n))

        def ffn(win_d, wout_d, acol, sh_n, ghcol, extra, pre_out0, do_norm=True, post_tile=None):
            if do_norm:
                for t in range(NT):
                    ffn_norm(t, acol, sh_n)
            groups = [(0, 6), (6, 12), (12, 18), (18, 22)]
            slot_ctr = {'n': 0}
            at_slot = {}

            def in_pair(pr):
                s = ws_alloc()
                f0_ = pr * 2
                load_cols(s, 0, win_d, f0_ * 128, 256)
                load_cols(s, 1, win_d, DFF + f0_ * 128, 256)
                for fi in range(2):
                    f = f0_ + fi
                    sl = slot_ctr['n'] % 8
                    slot_ctr['n'] += 1
                    at_slot[f] = sl
                    for t in range(NT):
                        cols = slice(t * 512, (t + 1) * 512)
                        pg, pgk = ps_next()
                        pu, puk = ps_next()

                        def fn(e, s=s, fi=fi, cols=cols, pg=pg, pu=pu):
                            ins = None
                            for k in range(KD):
                                ins = e.matmul(pg[:, :], lhsT=wslot(s)[:, k, fi * 128:(fi + 1) * 128], rhs=A_bf[:, k, cols],
                                               start=(k == 0), stop=(k == KD - 1))
                            for k in range(KD):
                                ins = e.matmul(pu[:, :], lhsT=wslot(s)[:, k, 256 + fi * 128:256 + (fi + 1) * 128], rhs=A_bf[:, k, cols],
                                               start=(k == 0), stop=(k == KD - 1))
                            return ins
                        P.add('pe', fn, reads=wsk(s, [fi, 2 + fi]) + [('hT', k, t) for k in range(KD)], writes=[pgk, puk])
                        tm, tmk = tmp_next()
                        P.add('act', lambda e, tm=tm, pg=pg: e.activation(out=tm, in_=pg[:, :], func=AF.Silu), reads=[pgk], writes=[tmk])
                        P.add('dve', lambda e, tm=tm, pu=pu, sl=sl, cols=cols: e.tensor_tensor(out=B_bf[:, sl, cols], in0=tm, in1=pu[:, :], op=ALU.mult),
                              reads=[tmk, puk], writes=[('aT', sl, t)])

            def load_wout(g):
                fa, fb = groups[g]
                for i in range((fb - fa) // 2):
                    dst = wo_buf[g % 2][:, 2 * i:2 * i + 2, :]
                    srcv = wout_d[(fa + 2 * i) * 128:(fa + 2 * i + 2) * 128, :].rearrange("(f p) d -> p f d", p=128)
                    P.add('pool', lambda e, dst=dst, srcv=srcv: e.dma_start(out=dst, in_=srcv),
                          writes=[('wo', g % 2, i)], dma_key=f'wo{g % 2}_{i}')

            def out_proj(g):
                fa, fb = groups[g]
                for t in range(NT):
                    cols = slice(t * 512, (t + 1) * 512)
                    for d in range(KD):
                        po, pok = ps_next()

                        def fn(e, d=d, cols=cols, po=po):
                            ins = None
                            for f in range(fa, fb):
                                ins = e.matmul(po[:, :], lhsT=wo_buf[g % 2][:, f - fa, d * 128:(d + 1) * 128], rhs=B_bf[:, at_slot[f], cols],
                                               start=(f == fa), stop=(f == fb - 1))
                            return ins
                        P.add('pe', fn, reads=[('wo', g % 2, i) for i in range((fb - fa) // 2)] + [('aT', at_slot[f], t) for f in range(fa, fb)],
                              writes=[pok])
                        P.add('dve', lambda e, d=d, cols=cols, po=po: e.scalar_tensor_tensor(out=xT[:, d, cols], in0=po[:, :], scalar=prm[:, ghcol + d:ghcol + d + 1],
                                                                                            in1=xT[:, d, cols], op0=ALU.mult, op1=ALU.add),
                              reads=[pok, ('prm', ghcol), ('xT', d, t)], writes=[('xT', d, t)])
                    if g == 3 and post_tile is not None:
                        post_tile(t)

            pairs_of = [list(range(a // 2, b // 2)) for a, b in groups]

            def pair_and_extra(pr):
                in_pair(pr)
                if extra:
                    extra.pop(0)()
            for g in range(4):
                prs = pairs_of[g]
                if g == 0:
                    pair_and_extra(prs[0])
                load_wout(g)
                for pr in prs[1:]:
                    pair_and_extra(pr)
                if g + 1 < 4:
                    pair_and_extra(pairs_of[g + 1][0])
                if g == 0 and pre_out0 is not None:
                    pre_out0()
                if g == 3:
                    while extra:
                        extra.pop(0)()
                out_proj(g)
            while extra:
                extra.pop(0)()

        yT = A_bf

        def cw(kk, j):
            return vecs[:, V_CW + kk * 4 + j: V_CW + kk * 4 + j + 1]

        E1, E1K = TMP[:, 4, :], ('tmp', 4)
        E2, E2K = TMP[:, 5, :], ('tmp', 5)
        D_hTt0 = hTt
        fl = vecs[:, V_FLAG:V_FLAG + 1]
        PREFIX_OFF = {'m231', 'm3', 'm4s', 'm4c', 'm4o', 'm5'}

        gn = vecs[:, V_GN:V_GN + 1]

        def gate_sq(t, h):
            oT = oTb[t % 2]
            hh, b = h % 2, h % 2
            okeys = [('oT', hh, t % 2, c) for c in range(4)]
            P.add('act', lambda e: e.activation(out=sqt[:, b, :], in_=oT[:, h, :], func=AF.Square), reads=okeys, writes=[('sqt', b)])

        def gate_rest(t, h):
            cols = slice(t * 512, (t + 1) * 512)
            oT = oTb[t % 2]
            hh, b = h % 2, h % 2
            okeys = [('oT', hh, t % 2, c) for c in range(4)]
            ps2, ps2k = ps_next()
            P.add('pe', lambda e: e.matmul(ps2[:, :], lhsT=ones_bf[:, :], rhs=sqt[:, b, :], start=True, stop=True),
                  reads=[('sqt', b), 'ones'], writes=[ps2k])
            rs, rsk = tmp_next()
            P.add('act', lambda e: e.activation(out=rs, in_=ps2[:, :], func=AF.Ln, scale=1.0 / 128, bias=prm[:, P_EPS:P_EPS + 1]),
                  reads=[ps2k, 'eps'], writes=[rsk])
            P.add('act', lambda e: e.activation(out=rs, in_=rs, func=AF.Exp, scale=-0.5), reads=[rsk], writes=[rsk])
            P.add('dve', lambda e: e.scalar_tensor_tensor(out=rs, in0=oT[:, h, :], scalar=gn, in1=rs, op0=ALU.mult, op1=ALU.mult),
                  reads=[rsk, 'vecs'] + okeys, writes=[rsk])
            P.add('dve', lambda e: e.tensor_tensor(out=yT[:, 4 + h, cols], in0=rs, in1=yT[:, 4 + h, cols], op=ALU.mult),
                  reads=[rsk, ('yT', 4 + h, t)], writes=[('yT', 4 + h, t)])

        def gate_norm(t):
            for h in range(4):
                gate_sq(t, h)
                gate_rest(t, h)

        hbufs = [hTt, hTt1]

        def norm_for(t):
            hb = hbufs[t % 2]
            norm_tile(t, lambda k: hb[:, k, :], lambda k: ('hTt', t % 2, k), P_A2, 3, ada_keys(3))

        def mixer_tiles(prefix, skip_norm0=False, last_hook=None):
            P.off = PREFIX_OFF if prefix else set()
            P.stage('mix_tiles')
            P.fence()
            ws_state['nslots'] = 3
            if prefix:
                P.add('pool', lambda e: e.dma_start(out=wgk[:, :, :], in_=wmi_d.rearrange("(k p) c -> p k c", p=128)[:, :, 3072:3088]),
                      writes=['wgk'], dma_key='wgk')
            P.add('dve', lambda e: e.memset(gk_aug[:, :], 1.0), writes=['gkaug'])
            skeys = [('S', 0), ('S', 1), ('Sbf', 0, 0), ('Sbf', 0, 1), ('Sbf', 1, 0), ('Sbf', 1, 1)]
            if prefix:
                P.add('dve', lambda e: e.memset(S_sb[:, :, :], 0.0), writes=[('S', 0), ('S', 1)])
                P.add('dve', lambda e: e.memset(S_bf[:, :, :], 0.0), writes=skeys[2:])
                P.add('dve', lambda e: e.memset(prm[:, P_UH:P_UH + 8], 0.0), writes=['uhalo'])
            else:
                P.add('dve', lambda e: e.memset(S_bf[:, 2:4, :], 0.0), writes=[('Sbf', 1, 0), ('Sbf', 1, 1)])
                for p in range(2):
                    P.add('dve', lambda e, p=p: e.tensor_scalar(out=S_sb[:, p, :], in0=spst[:, p * 256:(p + 1) * 256], scalar1=fl, scalar2=None, op0=ALU.mult),
                          reads=['spstA', 'vecs'], writes=[('S', p)])
                    P.add('dve', lambda e, p=p: e.tensor_scalar(out=S_bf[:, p, :], in0=spst[:, p * 256:(p + 1) * 256], scalar1=fl, scalar2=None, op0=ALU.mult),
                          reads=['spstA', 'vecs'], writes=[('Sbf', 0, p)])
                P.add('dve', lambda e: e.tensor_scalar(out=prm[:, P_UH:P_UH + 8], in0=spst[:, 512:520], scalar1=fl, scalar2=None, op0=ALU.mult),
                      reads=['spstB', 'vecs'], writes=['uhalo'])
            def tile_body(t):
                cols = slice(t * 512, (t + 1) * 512)
                hTt = hbufs[t % 2]
                hkeys = [('hTt', t % 2, k) for k in range(KD)]
                oT = oTb[t % 2]
                P.stage('m1')
                pgk_, pgkk = ps_next()

                def fn(e, pgk_=pgk_):
                    ins = None
                    for k in range(KD):
                        ins = e.matmul(pgk_[0:16, :], lhsT=wgk[:, k, :], rhs=hTt[:, k, :], start=(k == 0), stop=(k == KD - 1))
                    return ins
                P.add('pe', fn, reads=['wgk'] + hkeys, writes=[pgkk])
                P.add('act', lambda e, pgk_=pgk_: e.activation(out=gk_aug[0:16, :], in_=pgk_[0:16, :], func=AF.Identity), reads=[pgkk, 'gkaug'], writes=['gkaug'])
                s_qk = ws_alloc()
                if not prefix:
                    load_cols(s_qk, 0, wmi_d, 1536, 256)
                load_cols(s_qk, 1, wmi_d, 1792, 256)
                s_v = ws_alloc()
                load_cols(s_v, 0, wmi_d, 2048, 256)
                load_cols(s_v, 1, wmi_d, 2304, 256)
                for c in range(4):
                    gc = t * 4 + c
                    cc = slice(c * 128, (c + 1) * 128)
                    P.stage('m21')
                    pz, pzk = ps_next()
                    P.add('pe', lambda e, pz=pz, cc=cc: e.matmul(pz[:, 0:256], lhsT=gk_aug[0:17, cc], rhs=wgk2[0:17, :], start=True, stop=True),
                          reads=['gkaug', 'wgk2'], writes=[pzk])
                    sp_sb, spk = sp_bufs[gc % NB], ('sp', gc % NB)
                    er_sb, erk = er_bufs[gc % NB], ('er', gc % NB)
                    P.add('act', lambda e, pz=pz, sp_sb=sp_sb: e.activation(out=sp_sb, in_=pz[:, 0:256], func=AF.Exp, scale=-1.0), reads=[pzk], writes=[spk])
                    P.add('act', lambda e, sp_sb=sp_sb: e.activation(out=sp_sb, in_=sp_sb, func=AF.Ln, bias=1.0, scale=1.0), reads=[spk], writes=[spk])
                    P.stage('m25')
                    pv, pvk = ps_next()

                    def fn(e, pv=pv, cc=cc, s_v=s_v):
                        ins = None
                        for k in range(KD):
                            ins = e.matmul(pv[:, :], lhsT=hTt[:, k, cc], rhs=wslot(s_v)[:, k, :], start=(k == 0), stop=(k == KD - 1))
                        return ins
                    P.add('pe', fn, reads=wsk(s_v, range(4)) + hkeys, writes=[pvk])
                    P.add('dve', lambda e, pv=pv, c=c: e.tensor_copy(out=vtok[:, c, :], in_=pv[:, :]), reads=[pvk], writes=[('vtok', c)])
                    P.stage('m22')
                    pb, pbk = ps_next()

                    def fn(e, pb=pb, sp_sb=sp_sb):
                        e.matmul(pb[:, 0:128], lhsT=sp_sb[:, 0:128], rhs=tris[:, :], start=True, stop=True)
                        e.matmul(pb[:, 128:256], lhsT=sp_sb[:, 128:256], rhs=tris[:, :], start=True, stop=True)
                        return e.matmul(pb[:, 256:512], lhsT=triu[:, :], rhs=sp_sb[:, :], start=True, stop=True)
                    P.add('pe', fn, reads=[spk, 'tris', 'triu'], writes=[pbk])
                    pb3 = pb[:, 0:256].rearrange("p (a t) -> p a t", a=2)
                    P.stage('m231')
                    P.add('act', lambda e, pb3=pb3, cc=cc: e.activation(out=bT_sb[:, :, cc], in_=pb3, func=AF.Identity), reads=[pbk], writes=[('bT', c)])
                    P.stage('m233')
                    P.add('act', lambda e, pb3=pb3, c=c: e.activation(out=small[:, 2 * c:2 * c + 2].rearrange("p (a o) -> p a o", o=1), in_=pb3[:, :, 127:128], func=AF.Exp),
                          reads=[pbk], writes=[('ebl', c)])
                    P.stage('m234')
                    P.add('act', lambda e, pb=pb, er_sb=er_sb: e.activation(out=er_sb, in_=pb[:, 256:512], func=AF.Exp), reads=[pbk], writes=[erk])
                    P.stage('m24')
                    pk, pkk = ps_next()

                    def fn(e, pk=pk, cc=cc, s_qk=s_qk):
                        ins = None
                        for k in range(KD):
                            ins = e.matmul(pk[:, 0:256], lhsT=hTt[:, k, cc], rhs=wslot(s_qk)[:, k, 256:512], start=(k == 0), stop=(k == KD - 1))
                        return ins
                    P.add('pe', fn, reads=wsk(s_qk, [2, 3]) + hkeys, writes=[pkk])
                    P.add('dve', lambda e, pk=pk, c=c, er_sb=er_sb: e.tensor_tensor(out=khat[:, c, :], in0=pk[:, 0:256], in1=er_sb, op=ALU.mult),
                          reads=[pkk, erk], writes=[('khat', c)])
                P.stage('m3')
                bkeys = [('bT', c) for c in range(4)]
                for p in range(2):
                    pq, pqk = ps_next()

                    def fn(e, pq=pq, p=p, s_qk=s_qk):
                        ins = None
                        for k in range(KD):
                            ins = e.matmul(pq[:, :], lhsT=wslot(s_qk)[:, k, p * 128:(p + 1) * 128], rhs=hTt[:, k, :], start=(k == 0), stop=(k == KD - 1))
                        return ins
                    P.add('pe', fn, reads=wsk(s_qk, [p]) + hkeys, writes=[pqk])
                    pk2, pk2k = ps_next()

                    def fn(e, pk2=pk2, p=p, s_qk=s_qk):
                        ins = None
                        for k in range(KD):
                            ins = e.matmul(pk2[:, :], lhsT=wslot(s_qk)[:, k, 256 + p * 128:256 + (p + 1) * 128], rhs=hTt[:, k, :], start=(k == 0), stop=(k == KD - 1))
                        return ins
                    P.add('pe', fn, reads=wsk(s_qk, [2 + p]) + hkeys, writes=[pk2k])
                    P.add('act', lambda e, p=p: e.activation(out=E1, in_=bT_sb[:, p, :], func=AF.Exp), reads=bkeys, writes=[E1K])
                    P.add('dve', lambda e, p=p, pq=pq: e.scalar_tensor_tensor(out=qtT[:, p, :], in0=pq[:, :], scalar=0.125, in1=E1, op0=ALU.mult, op1=ALU.mult),
                          reads=[pqk, E1K], writes=[('qt', p)])
                    P.add('act', lambda e, p=p: e.activation(out=E1, in_=bT_sb[:, p, :], func=AF.Exp, scale=-1.0), reads=bkeys, writes=[E1K])
                    P.add('dve', lambda e, p=p, pk2=pk2: e.tensor_tensor(out=ktT[:, p, :], in0=pk2[:, :], in1=E1, op=ALU.mult),
                          reads=[pk2k, E1K], writes=[('kt', p)])
                if t + 1 < NT:
                    P.stage('m1')
                    norm_for(t + 1)
                elif last_hook is not None:
                    last_hook()
                for c in range(4):
                    gc = t * 4 + c
                    cc = slice(c * 128, (c + 1) * 128)
                    gcols = slice(t * 512 + c * 128, t * 512 + (c + 1) * 128)
                    P.stage('m4s')
                    for hh in range(2):
                        pscb, psck = ps_next()

                        def fn(e, hh=hh, pscb=pscb, cc=cc):
                            ins = None
                            for p in range(2):
                                ins = e.matmul(pscb[:, p * 128:(p + 1) * 128], lhsT=ktT[hh * 64:(hh + 1) * 64, p, cc], rhs=qtT[hh * 64:(hh + 1) * 64, p, cc],
                                               start=True, stop=True)
                            return ins
                        P.add('pe', fn, reads=[('kt', 0), ('kt', 1), ('qt', 0), ('qt', 1)], writes=[psck])
                        P.add('dve', lambda e, hh=hh, pscb=pscb, c=c: e.tensor_tensor(out=scm[:, (c % 2) * 2 + hh, :], in0=pscb[:, 0:256], in1=mask[:, :], op=ALU.mult),
                              reads=[psck, 'mask'], writes=[('scm', c % 2, hh)])
                    if t > 0:
                        P.stage('m5')
                        gate_sq(t - 1, c)
                    P.stage('m4c')
                    j = c
                    s = ws_alloc()
                    load_cols(s, 0, wmi_d, j * 128, 128, off=0)
                    load_cols(s, 0, wmi_d, 512 + j * 128, 128, off=128)
                    load_cols(s, 1, wmi_d, 1024 + j * 128, 128, off=0)
                    load_cols(s, 1, wmi_d, 2560 + j * 128, 128, off=128)
                    pp = [ps_next() for _ in range(4)]
                    for i4 in range(4):
                        pbank, pkey = pp[i4]

                        def fn(e, pbank=pbank, i4=i4, s=s):
                            ins = None
                            for k in range(KD):
                                ins = e.matmul(pbank[:, :], lhsT=wslot(s)[:, k, i4 * 128:(i4 + 1) * 128], rhs=hTt[:, k, :], start=(k == 0), stop=(k == KD - 1))
                            return ins
                        P.add('pe', fn, reads=wsk(s, [i4]) + hkeys, writes=[pkey])
                    (pcb, pcbk), (pcc, pcck), (pcv, pcvk), (pg_, pgk2) = pp
                    P.add('act', lambda e, pcc=pcc: e.activation(out=E1, in_=pcc[:, :], func=AF.Identity), reads=[pcck], writes=[E1K])
                    P.add('act', lambda e, j=j: e.activation(out=ubuf[:, 0:2], in_=prm[:, P_UH + 2 * j:P_UH + 2 * j + 2], func=AF.Identity), reads=['uhalo'], writes=['ubuf'])
                    P.add('dve', lambda e, pcv=pcv: e.tensor_tensor(out=ubuf[:, 2:514], in0=E1, in1=pcv[:, :], op=ALU.mult), reads=[E1K, pcvk, 'ubuf'], writes=['ubuf'])
                    P.add('act', lambda e, j=j: e.activation(out=E1, in_=ubuf[:, 2:514], func=AF.Identity, scale=cw(2, j)), reads=['ubuf', 'vecs'], writes=[E1K])
                    P.add('dve', lambda e, j=j: e.scalar_tensor_tensor(out=E2, in0=ubuf[:, 1:513], scalar=cw(1, j), in1=E1, op0=ALU.mult, op1=ALU.add),
                          reads=['ubuf', E1K, 'vecs'], writes=[E2K])
                    P.add('dve', lambda e, j=j: e.scalar_tensor_tensor(out=E1, in0=ubuf[:, 0:512], scalar=cw(0, j), in1=E2, op0=ALU.mult, op1=ALU.add),
                          reads=['ubuf', E2K, 'vecs'], writes=[E1K])
                    P.add('dve', lambda e, j=j, pcb=pcb, cols=cols: e.tensor_tensor(out=yT[:, j, cols], in0=pcb[:, :], in1=E1, op=ALU.mult), reads=[pcbk, E1K], writes=[('yT', j, t)])
                    P.add('act', lambda e, j=j: e.activation(out=prm[:, P_UH + 2 * j:P_UH + 2 * j + 2], in_=ubuf[:, 512:514], func=AF.Identity), reads=['ubuf'], writes=['uhalo'])
                    P.add('act', lambda e, j=j, pg_=pg_, cols=cols: e.activation(out=yT[:, 4 + j, cols], in_=pg_[:, :], func=AF.Silu), reads=[pgk2], writes=[('yT', 4 + j, t)])
                    if t > 0:
                        P.stage('m5')
                        gate_rest(t - 1, c)
                    P.stage('m4o')
                    sbuf_i = gc % 2
                    for hh in range(2):
                        po, pok = ps_next()

                        def fn(e, hh=hh, po=po, c=c, cc=cc, sbuf_i=sbuf_i):
                            ins = None
                            for p in range(2):
                                h = 2 * p + hh
                                e.matmul(po[:, p * 128:(p + 1) * 128], lhsT=vtok[:, c, h * 128:(h + 1) * 128], rhs=scm[:, (c % 2) * 2 + hh, p * 128:(p + 1) * 128],
                                         start=True, stop=False)
                                ins = e.matmul(po[:, p * 128:(p + 1) * 128], lhsT=S_bf[hh * 64:(hh + 1) * 64, sbuf_i * 2 + p, hh * 128:(hh + 1) * 128],
                                               rhs=qtT[hh * 64:(hh + 1) * 64, p, cc], start=False, stop=True)
                            return ins
                        P.add('pe', fn, reads=[('vtok', c), ('scm', c % 2, hh), ('Sbf', sbuf_i, 0), ('Sbf', sbuf_i, 1), ('qt', 0), ('qt', 1)], writes=[pok])
                        P.add('act', lambda e, hh=hh, po=po, cc=cc: e.activation(out=oT[:, hh:4:2, cc], in_=po[:, 0:256].rearrange("p (a t) -> p a t", a=2), func=AF.Identity),
                              reads=[pok], writes=[('oT', hh, t % 2, c)])
                    P.stage('m4u')
                    pu_, puk_ = ps_next()

                    def fn(e, pu_=pu_, c=c):
                        e.matmul(pu_[:, 0:256], lhsT=khat[:, c, 0:128], rhs=vtok[:, c, 0:256], start=True, stop=True)
                        return e.matmul(pu_[:, 256:512], lhsT=khat[:, c, 128:256], rhs=vtok[:, c, 256:512], start=True, stop=True)
                    P.add('pe', fn, reads=[('khat', c), ('vtok', c)], writes=[puk_])
                    for p in range(2):
                        P.add('dve', lambda e, p=p, pu_=pu_, c=c: e.scalar_tensor_tensor(out=S_sb[:, p, :], in0=S_sb[:, p, :], scalar=small[:, 2 * c + p:2 * c + p + 1],
                                                                                         in1=pu_[:, p * 256:(p + 1) * 256], op0=ALU.mult, op1=ALU.add),
                              reads=[('S', p), puk_, ('ebl', c)], writes=[('S', p)])
                        P.add('act', lambda e, p=p, sbuf_i=sbuf_i: e.activation(out=S_bf[:, (1 - sbuf_i) * 2 + p, :], in_=S_sb[:, p, :], func=AF.Identity),
                              reads=[('S', p)], writes=[('Sbf', 1 - sbuf_i, p)])
            if not skip_norm0:
                P.stage('m1')
                norm_for(0)
            for t in range(NT):
                tile_body(t)
            P.stage('m5')
            gate_norm(NT - 1)
            P.off = set()
            P.stage('mix_tiles')
            if prefix:
                ws_state['nslots'] = 2

        def prefix_tail():
            hTt = [D_hTt0, hTt1][(NT - 1) % 2]
            hkeys = [('hTt', (NT - 1) % 2, k) for k in range(KD)]
            for j in range(4):
                s = ws_alloc()
                load_cols(s, 0, wmi_d, 512 + j * 128, 128, off=0)
                load_cols(s, 0, wmi_d, 1024 + j * 128, 128, off=128)
                pc_, pck_ = ps_next()

                def fn(e, s=s, pc_=pc_):
                    ins = None
                    for i2 in range(2):
                        for k in range(KD):
                            ins = e.matmul(pc_[:, 2 * i2:2 * i2 + 2], lhsT=wslot(s)[:, k, i2 * 128:(i2 + 1) * 128], rhs=hTt[:, k, 510:512],
                                           start=(k == 0), stop=(k == KD - 1))
                    return ins
                P.add('pe', fn, reads=wsk(s, [0, 1]) + hkeys, writes=[pck_])
                P.add('act', lambda e, pc_=pc_, j=j: e.activation(out=small[:, 48 + 2 * j:50 + 2 * j], in_=pc_[:, 0:2], func=AF.Identity), reads=[pck_], writes=[('ut', j)])
                P.add('dve', lambda e, pc_=pc_, j=j: e.tensor_tensor(out=spst[:, 512 + 2 * j:514 + 2 * j], in0=small[:, 48 + 2 * j:50 + 2 * j], in1=pc_[:, 2:4], op=ALU.mult),
                      reads=[pck_, ('ut', j)], writes=['spstB'])
            P.add('act', lambda e: e.activation(out=spst[:, 0:512].rearrange("p (a t) -> p a t", a=2), in_=S_sb[:, :, :], func=AF.Identity),
                  reads=[('S', 0), ('S', 1)], writes=['spstA'])

        P.stage('ada0')
        for _ in range(4):
            ada_block()
        make_A(P_A1, 1, V_N1)
        P.stage('pre_ffn1')
        first_norm = lambda t: norm_for(0) if t == 0 else None
        ffn(w1i_d, w1o_d, P_A1, 0, P_GH1, [ada2] * 12, lambda: make_half(P_GH1, 2), post_tile=lambda t: (make_A(P_A2, 4, V_NM), norm_for(0)) if t == 0 else None)

        def load_x_and_norm1():
            P.stage('ffn1')
            for k in range(KD):
                P.add('sp', lambda e, k=k: e.dma_start(out=xT[:, k, :], in_=xT_d[k * 128:(k + 1) * 128, :]),
                      writes=[('xT', k, t) for t in range(NT)], dma_key=f'x{k}')
            for t in range(NT):
                ffn_norm(t, P_A1, 0)
        mixer_tiles(True, skip_norm0=True, last_hook=load_x_and_norm1)
        P.stage('pre_tail')
        prefix_tail()
        P.stage('ffn1')
        P.fence()
        ffn(w1i_d, w1o_d, P_A1, 0, P_GH1, [ada2] * 12, None, do_norm=False, post_tile=first_norm)
        mixer_tiles(False, skip_norm0=True)

        P.stage('mixout')
        make_A(P_A3, 7, V_N2)
        wm_slots = [ws_alloc(), ws_alloc()]
        for i, s in enumerate(wm_slots):
            dst = wslot(s)[:, :, :].rearrange("p k c -> p (k c)").rearrange("p (k d) -> p k d", k=4)
            srcv = wmo_d[i * 512:(i + 1) * 512, :].rearrange("(k p) d -> p k d", p=128)
            P.add('pool', lambda e, dst=dst, srcv=srcv: e.dma_start(out=dst, in_=srcv), writes=wsk(s, range(4)), dma_key=f'wm{s}')
        wm_view = [wslot(s)[:, :, :].rearrange("p k c -> p (k c)").rearrange("p (k d) -> p k d", k=4) for s in wm_slots]
        for t in range(NT):
            cols = slice(t * 512, (t + 1) * 512)
            for d in range(KD):
                po, pok = ps_next()

                def fn(e, d=d, cols=cols, po=po):
                    ins = None
                    for kc in range(8):
                        ins = e.matmul(po[:, :], lhsT=wm_view[kc // 4][:, kc % 4, d * 128:(d + 1) * 128], rhs=yT[:, kc, cols], start=(kc == 0), stop=(kc == 7))
                    return ins
                P.add('pe', fn, reads=[('ws', s, q) for s in wm_slots for q in range(4)] + [('yT', kc, t) for kc in range(8)], writes=[pok])
                P.add('dve', lambda e, d=d, cols=cols, po=po: e.scalar_tensor_tensor(out=xT[:, d, cols], in0=po[:, :], scalar=ada[:, 40 + d:41 + d],
                                                                                    in1=xT[:, d, cols], op0=ALU.mult, op1=ALU.add),
                      reads=[pok, ('xT', d, t)] + ada_keys(5), writes=[('xT', d, t)])
            ffn_norm(t, P_A3, 6)

        def final_tile(t):
            cols = slice(t * 512, (t + 1) * 512)
            ps, psk = ps_next()
            for k in range(KD):
                b = k % 2
                P.add('act', lambda e, k=k, b=b, cols=cols: e.activation(out=sqt[:, b, :], in_=xT[:, k, cols], func=AF.Square),
                      reads=[('xT', k, t)], writes=[('sqt', b)])
                P.add('pe', lambda e, k=k, b=b, ps=ps: e.matmul(ps[:, :], lhsT=ones_bf[:, :], rhs=sqt[:, b, :], start=(k == 0), stop=(k == KD - 1)),
                      reads=[('sqt', b), 'ones'], writes=[psk])
            rs, rsk = tmp_next()
            P.add('act', lambda e, rs=rs, ps=ps: e.activation(out=rs, in_=ps[:, :], func=AF.Ln, scale=1.0 / D, bias=prm[:, P_EPS:P_EPS + 1]),
                  reads=[psk, 'eps'], writes=[rsk])
            P.add('act', lambda e, rs=rs: e.activation(out=rs, in_=rs, func=AF.Exp, scale=-0.5), reads=[rsk], writes=[rsk])
            for k in range(KD):
                P.add('dve', lambda e, k=k, rs=rs, cols=cols: e.scalar_tensor_tensor(out=xT[:, k, cols], in0=xT[:, k, cols], scalar=vecs[:, V_NF + k:V_NF + k + 1],
                                                                                    in1=rs, op0=ALU.mult, op1=ALU.mult),
                      reads=[('xT', k, t), rsk, 'vecs'], writes=[('xT', k, t)])
                P.add('sp', lambda e, k=k, cols=cols: e.dma_start(out=out_d[k * 128:(k + 1) * 128, cols], in_=xT[:, k, cols]),
                      reads=[('xT', k, t)], writes=[('out', k, t)], dma_key=f'o{k}')

        P.stage('ffn2')
        P.fence()
        ws_state['nslots'] = 2
        ffn(w2i_d, w2o_d, P_A3, 6, P_GH3, [ada2] * 4, lambda: make_half(P_GH3, 8), do_norm=False, post_tile=final_tile)
        P.emit(nc, st)
    return nc


_CACHE = {}


def kernel(x, c, w_ada, b_ada, norm_ffn1, w_ffn1_in, w_ffn1_out, norm_mix, w_mix_in, conv_w, w_gk2, b_gk,
           gla_norm, w_mix_out, norm_ffn2, w_ffn2_in, w_ffn2_out, norm_final):
    f = lambda a: np.ascontiguousarray(np.asarray(a, dtype=np.float32))
    x = f(x)
    B, T, _ = x.shape
    TT = T // 2
    if T not in _CACHE:
        _CACHE[T] = build(T)
    nc = _CACHE[T]

    def fm(v):
        v = f(v).reshape(-1, 128)
        return v.T

    jj, ii = np.meshgrid(np.arange(128), np.arange(128), indexing='ij')
    tris = np.where(jj <= ii, -1.0 / 16.0, 0.0).astype(np.float32)
    triu = np.where(jj > ii, -1.0 / 16.0, 0.0).astype(np.float32)
    mask = np.tile(np.where(jj <= ii, 1.0, 0.0).astype(np.float32), (1, 2))
    wgk2a = np.concatenate([f(w_gk2)[0], f(b_gk)[0][None, :]], axis=0)
    shared = {
        "w_ada": f(w_ada)[0], "w_ffn1_in": f(w_ffn1_in)[0], "w_ffn1_out": f(w_ffn1_out)[0],
        "w_mix_in": f(w_mix_in)[0], "wgk2a": f(wgk2a), "w_mix_out": f(w_mix_out)[0],
        "w_ffn2_in": f(w_ffn2_in)[0], "w_ffn2_out": f(w_ffn2_out)[0],
        "tris": tris, "triu": triu, "mask": mask,
    }
    cwv = f(conv_w)[0]
    in_maps = []
    for core in range(NCORES):
        b, half = core // 2, core % 2
        vecs = np.zeros((128, NV), np.float32)
        vecs[:, 0:8] = fm(f(c)[b])
        vecs[:, 8:80] = fm(f(b_ada)[0])
        vecs[:, 80:88] = fm(f(norm_ffn1)[0])
        vecs[:, 88:96] = fm(f(norm_mix)[0])
        vecs[:, 96:104] = fm(f(norm_ffn2)[0])
        vecs[:, 104:112] = fm(f(norm_final))
        for kk in range(3):
            vecs[:, 112 + kk * 4:112 + kk * 4 + 4] = fm(cwv[kk])
        vecs[:, 124] = f(gla_norm)[0]
        vecs[:, 125] = float(half)
        m = dict(shared)
        m["xT"] = np.ascontiguousarray(x[b, half * TT:(half + 1) * TT, :].T)
        m["xTp"] = np.ascontiguousarray(x[b, 0:TT, :].T) if half == 1 else np.zeros((D, TT), np.float32)
        m["vecs"] = vecs
        in_maps.append(m)
    res = run_bass_kernel_spmd(nc, in_maps, core_ids=list(range(NCORES)))
    out = np.empty((B, T, D), np.float32)
    for core in range(NCORES):
        b, half = core // 2, core % 2
        out[b, half * TT:(half + 1) * TT, :] = res.results[core]["outT"].T
    return out
```
